# Optimizing a Trainium2 kernel written in Bass

```python
import jax, jax.numpy as jnp
from jax import lax
import numpy as np

D_MODEL = 1024
BATCH = 32
SEQ = 2048
DEPTH = 4
DEC_BATCH = 16
DEC_SEQ = 4096
PAST_LEN = 128

GRID_W = 64
N_MIXERS = 2
POOL_WINDOWS = (2, 4, 8, 16)
N_POOL_GROUPS = 4
POOL_GROUP = D_MODEL // N_POOL_GROUPS
N_HEADS = 16
HEAD_DIM = D_MODEL // N_HEADS
WIN_ROWS = 8
WIN_COLS = 16
Q_BLOCK_COLS = 16
K_BLOCK_COLS = Q_BLOCK_COLS + WIN_COLS
N_COL_BLOCKS = GRID_W // Q_BLOCK_COLS
D_FF = 4 * D_MODEL
RMS_EPS = 1e-6
NEG_INF = -1e30
N_POOL_LAYERS = (DEPTH + 1) // 2
N_ATTN_LAYERS = DEPTH // 2

kernel_name = "hybrid_pool_natten_encoder"


def rms_norm(x, g):
    xf = x.astype(jnp.float32)
    y = xf * lax.rsqrt(jnp.mean(xf * xf, axis=-1, keepdims=True) + RMS_EPS)
    return (y * g.astype(jnp.float32)).astype(x.dtype)


def pool_mixer(h, w_groups, scale):
    b, s, _ = h.shape
    hg = h.reshape(b, s, N_POOL_GROUPS, POOL_GROUP).astype(jnp.float32)
    cs = jnp.concatenate([jnp.zeros((b, 1, N_POOL_GROUPS, POOL_GROUP), jnp.float32),
                          jnp.cumsum(hg, axis=1)], axis=1)
    t = jnp.arange(s)[:, None]
    w = jnp.array(POOL_WINDOWS, dtype=jnp.int32)[None, :]
    lo = jnp.clip(t - w // 2, 0, s - 1)
    hi = jnp.clip(t + w // 2 - 1, 0, s - 1)
    g_idx = jnp.arange(N_POOL_GROUPS)[None, :]
    window_sum = cs[:, hi + 1, g_idx, :] - cs[:, lo, g_idx, :]
    count = (hi - lo + 1).astype(jnp.float32)[None, :, :, None]
    pooled = (window_sum / count - hg).astype(h.dtype)
    mixed = jnp.einsum('bsgc,gcd->bsgd', pooled, w_groups)
    return mixed.reshape(b, s, D_MODEL) * scale


def _column_structure():
    qc = np.arange(GRID_W).reshape(N_COL_BLOCKS, Q_BLOCK_COLS)
    kb_start = np.clip(np.arange(N_COL_BLOCKS) * Q_BLOCK_COLS - WIN_COLS // 2, 0, GRID_W - K_BLOCK_COLS)
    kc = kb_start[:, None] + np.arange(K_BLOCK_COLS)[None, :]
    win_start = np.clip(qc - WIN_COLS // 2, 0, GRID_W - WIN_COLS)
    col_valid = (kc[:, None, :] >= win_start[:, :, None]) & (kc[:, None, :] < win_start[:, :, None] + WIN_COLS)
    dc_idx = np.clip(kc[:, None, :] - qc[:, :, None] + WIN_COLS - 1, 0, 2 * WIN_COLS - 2)
    return kc, col_valid, dc_idx


def neighborhood_attention(h, w_qkv, w_o, rpb):
    b, s, _ = h.shape
    rows = s // GRID_W
    kr = min(WIN_ROWS, rows)
    qkv = h @ w_qkv
    q, k, v = jnp.split(qkv, 3, axis=-1)
    q = q.reshape(b, rows, GRID_W, N_HEADS, HEAD_DIM) * (HEAD_DIM ** -0.5)
    k = k.reshape(b, rows, GRID_W, N_HEADS, HEAD_DIM)
    v = v.reshape(b, rows, GRID_W, N_HEADS, HEAD_DIM)
    kc_np, col_valid_np, dc_idx_np = _column_structure()
    kc = jnp.asarray(kc_np)
    col_valid = jnp.asarray(col_valid_np)[:, :, None, :]
    dc_idx = jnp.asarray(dc_idx_np)[:, :, None, :]

    def row_block(args):
        r, q_row = args
        r0 = jnp.clip(r - kr // 2, 0, rows - kr)
        k_rows = lax.dynamic_slice_in_dim(k, r0, kr, axis=1)
        v_rows = lax.dynamic_slice_in_dim(v, r0, kr, axis=1)
        k_blk = k_rows[:, :, kc]
        v_blk = v_rows[:, :, kc]
        qb = q_row.reshape(b, N_COL_BLOCKS, Q_BLOCK_COLS, N_HEADS, HEAD_DIM)
        sc = jnp.einsum('bjqhd,brjchd->bhjqrc', qb, k_blk,
                        preferred_element_type=jnp.float32)
        dr_idx = (r0 + jnp.arange(kr) - r + WIN_ROWS - 1)[None, None, :, None]
        bias = rpb[:, dr_idx, dc_idx]
        sc = jnp.where(col_valid, sc + bias[None].astype(jnp.float32), NEG_INF)
        p = jax.nn.softmax(sc, axis=(-2, -1)).astype(v.dtype)
        o = jnp.einsum('bhjqrc,brjchd->bjqhd', p, v_blk)
        return o.reshape(b, GRID_W, D_MODEL)

    out = lax.map(row_block, (jnp.arange(rows), jnp.moveaxis(q, 1, 0)))
    out = jnp.moveaxis(out, 0, 1).reshape(b, s, D_MODEL)
    return out @ w_o


def trunk(x, norm_mix, pool_w, pool_scale, w_qkv, rpb, w_o, norm_mlp, w_up, w_down, norm_final):
    for i in range(DEPTH):
        h = rms_norm(x, norm_mix[i])
        j = i // N_MIXERS
        if i % N_MIXERS == 0:
            x = x + pool_mixer(h, pool_w[j], pool_scale[j])
        else:
            x = x + neighborhood_attention(h, w_qkv[j], w_o[j], rpb[j])
        h = rms_norm(x, norm_mlp[i])
        x = x + jnp.square(jax.nn.relu(h @ w_up[i])) @ w_down[i]
    return rms_norm(x, norm_final)


def setup_inputs(seed: int = 0) -> dict:
    key = jax.random.key(seed)
    ks = jax.random.split(key, 12)
    f32 = jnp.float32
    return {
        "x_prompt": jax.random.normal(ks[0], (BATCH, SEQ, D_MODEL), f32),
        "x_sample": jax.random.normal(ks[1], (DEC_BATCH, DEC_SEQ, D_MODEL), f32),
        "norm_mix": 1.0 + 0.02 * jax.random.normal(ks[2], (DEPTH, D_MODEL), f32),
        "pool_w": jax.random.normal(ks[3], (N_POOL_LAYERS, N_POOL_GROUPS, POOL_GROUP, POOL_GROUP), f32) * POOL_GROUP ** -0.5,
        "pool_scale": 1.0 + 0.02 * jax.random.normal(ks[4], (N_POOL_LAYERS, D_MODEL), f32),
        "w_qkv": jax.random.normal(ks[5], (N_ATTN_LAYERS, D_MODEL, 3 * D_MODEL), f32) * D_MODEL ** -0.5,
        "rpb": 0.1 * jax.random.normal(ks[6], (N_ATTN_LAYERS, N_HEADS, 2 * WIN_ROWS - 1, 2 * WIN_COLS - 1), f32),
        "w_o": jax.random.normal(ks[7], (N_ATTN_LAYERS, D_MODEL, D_MODEL), f32) * D_MODEL ** -0.5,
        "norm_mlp": 1.0 + 0.02 * jax.random.normal(ks[8], (DEPTH, D_MODEL), f32),
        "w_up": jax.random.normal(ks[9], (DEPTH, D_MODEL, D_FF), f32) * D_MODEL ** -0.5,
        "w_down": jax.random.normal(ks[10], (DEPTH, D_FF, D_MODEL), f32) * D_FF ** -0.5,
        "norm_final": 1.0 + 0.02 * jax.random.normal(ks[11], (D_MODEL,), f32),
    }


def reference(x_prompt, x_sample, norm_mix, pool_w, pool_scale, w_qkv, rpb, w_o, norm_mlp, w_up, w_down, norm_final):
    y_prompt = trunk(x_prompt, norm_mix, pool_w, pool_scale, w_qkv, rpb, w_o, norm_mlp, w_up, w_down, norm_final)
    y_sample = trunk(x_sample, norm_mix, pool_w, pool_scale, w_qkv, rpb, w_o, norm_mlp, w_up, w_down, norm_final)
    return (y_prompt, y_sample)
```

```python
import contextlib
import numpy as np
import concourse.bass as bass
import concourse.mybir as mybir
from concourse.bass_utils import run_bass_kernel_spmd

F32 = mybir.dt.float32
BF16 = mybir.dt.bfloat16
ALU = mybir.AluOpType
AF = mybir.ActivationFunctionType

NCORES = 8
D = 1024
NCH = 8
FF = 4096
TT = 512
DEPTH = 4
EPS = 1e-6
NEG = -1e30
TMAX = 2560
FS = 512
NSL = FF // FS
RPB_PAD = 128
RPB_L = 16 * 15 * 31 + 2 * RPB_PAD

ENGS = ("pe", "act", "dve", "pool", "sp")


class Op:
    __slots__ = ("eng", "fn", "deps", "marked", "count", "dma_sem", "dma_val", "is_nop")

    def __init__(self, eng, fn):
        self.eng = eng
        self.fn = fn
        self.deps = []
        self.marked = False
        self.count = None
        self.dma_sem = None
        self.dma_val = None
        self.is_nop = False


class Res:
    __slots__ = ("name", "w", "wd", "r", "rd")

    def __init__(self, name):
        self.name = name
        self.w = {}
        self.wd = []
        self.r = {}
        self.rd = []


class Prog:
    def __init__(self):
        self.q = {e: [] for e in ENGS}
        self.dma_counts = {}

    def op(self, eng, meth, args=(), kwargs=None, reads=(), writes=(), fresh=(), dma_sem=None, extra_deps=(),
           nop=False):
        o = Op(eng, (meth, tuple(args), kwargs or {}))
        o.is_nop = nop
        raw = []
        oth = []
        for r in reads:
            raw.extend(r.w.values())
            raw.extend(r.wd)
        for w in tuple(writes) + tuple(fresh):
            oth.extend(w.w.values())
            oth.extend(w.wd)
            oth.extend(w.r.values())
            oth.extend(w.rd)
        raw.extend(extra_deps)
        seen = set()
        for lst, is_raw in ((raw, True), (oth, False)):
            for d in lst:
                if d is None or id(d) in seen:
                    continue
                seen.add(id(d))
                if d.eng == eng and d.dma_sem is None:
                    if d.is_nop or eng == "pe" or eng == "sp":
                        continue
                    if not is_raw and dma_sem is None:
                        continue
                o.deps.append(d)
        for r in reads:
            if dma_sem is not None:
                r.rd.append(o)
            else:
                r.r[eng] = o
        for w in fresh:
            w.w = {}
            w.wd = []
            w.r = {}
            w.rd = []
        for w in tuple(writes) + tuple(fresh):
            if dma_sem is not None:
                w.wd.append(o)
            else:
                w.w[eng] = o
        if dma_sem is not None:
            o.dma_sem = dma_sem
            v = self.dma_counts.get(dma_sem, 0) + 16
            self.dma_counts[dma_sem] = v
            o.dma_val = v
        self.q[eng].append(o)
        return o

    def join(self, eng, res_list):
        o = self.op(eng, "nop", reads=res_list, nop=True)
        for r in res_list:
            r.w = {eng: o}
            r.wd = []
        return o

    def finalize(self):
        for e in ENGS:
            for o in self.q[e]:
                for d in o.deps:
                    if d.dma_sem is None:
                        d.marked = True
        for e in ENGS:
            c = 0
            for o in self.q[e]:
                if o.marked:
                    c += 1
                    o.count = c

    def emit(self, block, sems, dma_sems):
        self.finalize()
        prog = self

        def run(engname, engine):
            waited = {}
            for o in prog.q[engname]:
                for d in o.deps:
                    if d.dma_sem is not None:
                        key = ("dma", d.dma_sem)
                        val = d.dma_val
                        sem = dma_sems[d.dma_sem]
                    else:
                        key = d.eng
                        val = d.count
                        sem = sems[d.eng]
                    if waited.get(key, 0) >= val:
                        continue
                    waited[key] = val
                    engine.wait_ge(sem, val)
                meth, args, kwargs = o.fn
                ins = getattr(engine, meth)(*args, **kwargs)
                if o.dma_sem is not None:
                    ins.then_inc(dma_sems[o.dma_sem], 16)
                elif o.marked:
                    ins.then_inc(sems[engname], 1)

        @block.tensor
        def _(t):
            run("pe", t)

        @block.scalar
        def _(s):
            run("act", s)

        @block.vector
        def _(v):
            run("dve", v)

        @block.gpsimd
        def _(g):
            run("pool", g)

        @block.sync
        def _(s):
            run("sp", s)


def r0_local(rl, R, off, Rg):
    r0g = min(max(rl + off - 4, 0), Rg - 8)
    return min(max(r0g - off, 0), R - 8)


def attn_struct(R, off, Rg):
    pairs = []
    for p in range(R // 2):
        items = []
        for rl in range(R):
            r0 = r0_local(rl, R, off, Rg)
            k0, k1 = 2 * p, 2 * p + 1
            v0 = r0 <= k0 <= r0 + 7
            v1 = r0 <= k1 <= r0 + 7
            if not (v0 or v1):
                continue
            j = rl - k0 + 7
            assert 0 <= j <= 14
            sec = "11" if (v0 and v1) else ("10" if v0 else "01")
            items.append((rl, sec, j))
        rows = [it[0] for it in items]
        assert rows == list(range(rows[0], rows[-1] + 1))
        pairs.append(items)
    return pairs


SLOTS = {}
for _j in range(15):
    SLOTS[("11", _j)] = _j if _j <= 4 else (_j + 1 if _j <= 11 else _j + 2)
SLOTS[("10", 4)] = 5
SLOTS[("01", 12)] = 13
NSLOT = 17
TBLW = 2 * NSLOT * 64


def make_units():
    units = []
    for i in range(4):
        units.append(dict(src="xp", dst="yp", seq=i, tok0=0, R=32, off=0, Rg=32, out_lo=0))
    for i in range(2):
        units.append(dict(src="xs", dst="ys", seq=i, tok0=0, R=40, off=0, Rg=64, out_lo=0))
        units.append(dict(src="xs", dst="ys", seq=i, tok0=24 * 64, R=40, off=24, Rg=64, out_lo=8))
    return units


def host_consts():
    ident = np.eye(128, dtype=np.float32)
    jm = np.zeros((128, 128), np.float32)
    for k in range(128):
        jm[k, (k // 64) * 64 + 63 - (k % 64)] = 1.0
    m01 = np.zeros((128, 64), np.float32)
    for p in range(128):
        kc = 63 - (p % 64)
        for c in range(64):
            ws = min(max(c - 8, 0), 48)
            if ws <= kc < ws + 16:
                m01[p, c] = 1.0
    negm = ((1.0 - m01) * np.float32(NEG)).astype(np.float32)
    invc = np.zeros((128, 8, 16), np.float32)
    for c in range(8):
        w = 2 ** (c // 2 + 1)
        for j in range(8):
            invc[:, c, j] = 1.0 / min(w, j + w // 2)
            invc[:, c, 8 + j] = 1.0 / min(w, (8 - j) + w // 2)
    return dict(ident=ident, jmat=jm, m01=m01, negm=negm, invc=invc)


def build_program(units=None, nphases=2 * DEPTH, attdbg=9, ngroups=8):
    nc = bass.Bass("TRN2", target_bir_lowering=False)
    P = Prog()
    EPS_AP = EPS
    units = make_units() if units is None else units

    def din(name, shape):
        return nc.dram_tensor(name, list(shape), F32, kind="ExternalInput").ap()

    xsrc = {"xp": din("xp", [4, 2048, D]), "xs": din("xs", [2, 4096, D])}
    ydst = {"yp": nc.dram_tensor("yp", [4, 2048, D], F32, kind="ExternalOutput").ap(),
            "ys": nc.dram_tensor("ys", [2, 4096, D], F32, kind="ExternalOutput").ap()}
    w_up = din("w_up", [DEPTH, D, FF])
    w_down = din("w_down", [DEPTH, FF, D])
    w_qkv = din("w_qkv", [2, D, 3 * D])
    w_o = din("w_o", [2, D, D])
    pool_w = din("pool_w", [2, 4, 256, 256])
    gv_in = din("gv", [128, 11, 8])
    rpbx = din("rpbx", [2, RPB_L])
    ident_in = din("ident", [128, 128])
    jmat_in = din("jmat", [128, 128])
    m01_in = din("m01", [128, 64])
    negm_in = din("negm", [128, 64])
    invc_in = din("invc", [128, 8, 16])

    wup_b = nc.dram_tensor("wup_b", [DEPTH, D, FF], BF16).ap()
    wdn_b = nc.dram_tensor("wdn_b", [DEPTH, FF, D], BF16).ap()
    wqkv_b = nc.dram_tensor("wqkv_b", [2, D, 3 * D], BF16).ap()
    wo_b = nc.dram_tensor("wo_b", [2, D, D], BF16).ap()
    pw_b = nc.dram_tensor("pw_b", [2, 4, 256, 256], BF16).ap()
    tbl_b = nc.dram_tensor("tbl_b", [2, 8, 128, TBLW], BF16).ap()

    with contextlib.ExitStack() as es:
        def sb(name, shape, dt):
            return es.enter_context(nc.sbuf_tensor(name, list(shape), dt))

        XT = sb("XT", [128, NCH, TMAX], F32)
        HC = sb("HC", [128, NCH, TMAX], BF16)
        RING = sb("RING", [128, 2, 8192], BF16)
        GV = sb("GV", [128, 11, 8], F32)
        ONESM = sb("ONESM", [128, 128], BF16)
        IDF = sb("IDF", [128, 128], F32)
        JB = sb("JB", [128, 128], BF16)
        INVC = sb("INVC", [128, 8, 16], F32)
        SQ = sb("SQ", [128, 4, TT], BF16)
        RS = sb("RS", [128, 2, TT], F32)
        AT = sb("AT", [128, 2, 4, TT], BF16)
        RT = sb("RT", [128, 2, TT], F32)
        A0N = 16640
        A0 = sb("A0", [128, A0N], BF16)
        PS = [es.enter_context(nc.psum_tensor("PS%d" % i, [128, 512], F32)) for i in range(8)]

        sems = {e: es.enter_context(nc.semaphore("s_" + e)) for e in ENGS}
        dma_names = ["c_misc", "c_msk", "cast0", "cast1", "cast2", "cast3", "cst0", "cst1", "cst2", "cst3", "tblraw", "tblst", "ring0", "ring1",
                     "xs0", "xs1", "yst0", "yst1", "pw", "wob0", "wob1"]
        dma_sems = {n: es.enter_context(nc.semaphore("d_" + n)) for n in dma_names}
        block = es.enter_context(nc.Block())

        def a0_bf(off, n):
            assert off + n <= A0N
            return A0[:, off:off + n]

        def a0_f32(off, n):
            assert off % 2 == 0 and off + 2 * n <= A0N
            return A0[:, off:off + 2 * n].bitcast(F32)

        QT = a0_bf(0, TMAX)
        KT = a0_bf(2560, TMAX)
        OT = a0_bf(5120, TMAX)
        VV = a0_bf(7680, 20 * 192).rearrange("p (a b) -> p a b", b=192)
        PT = a0_bf(11520, 2 * 4 * 512).rearrange("p (s b n) -> p s b n", s=2, b=4)
        RD = a0_f32(15616, 512)
        YS = a0_f32(0, 2048).rearrange("p (s n) -> p s n", s=2)
        XS = a0_f32(12288, 2048).rearrange("p (s n) -> p s n", s=2)
        HF = a0_f32(4096, NCH * TT).rearrange("p (c n) -> p c n", c=NCH)
        PB2 = TT + 16
        RSTDP = a0_f32(0, PB2)
        HP2 = [a0_f32(2 * PB2 * (1 + i), PB2) for i in range(2)]
        WA2 = [a0_f32(2 * PB2 * (3 + i), PB2) for i in range(2)]
        WB2 = [a0_f32(2 * PB2 * (5 + i), PB2) for i in range(2)]
        po = 2 * PB2 * 7
        PL2 = a0_bf(po, 2 * 2 * TT).rearrange("p (b k n) -> p b k n", b=2, k=2)
        SV = a0_f32(po + 2048, 64).rearrange("p (c n) -> p c n", c=NCH)
        SQH = a0_bf(po + 2048 + 128, 64).rearrange("p (c n) -> p c n", c=NCH)
        PW = a0_bf(po + 2048 + 128 + 64, 2048)
        CB = a0_bf(0, 8192).rearrange("p (s n) -> p s n", s=2)
        RAW = a0_f32(8192, 15 * 64).rearrange("p (j n) -> p j n", j=15)
        TB = a0_bf(8192 + 1920, NSLOT * 64).rearrange("p (j n) -> p j n", j=NSLOT)
        so = 8192 + 1920 + NSLOT * 64
        M01 = a0_f32(so, 64)
        NEGM = a0_f32(so + 128, 64)
        JF = a0_f32(so + 256, 128)

        r_bank = [Res("bank%d" % i) for i in range(8)]
        r_xt = [[Res("xt%d_%d" % (c, t)) for t in range(TMAX // TT)] for c in range(NCH)]
        r_hc = [Res("hc%d" % t) for t in range(TMAX // TT)]
        r_ring = [Res("ring0"), Res("ring1")]
        r_const = Res("const")
        r_scr = Res("scr")
        r_sq = [Res("sq%d" % i) for i in range(4)]
        r_rs = [Res("rs%d" % i) for i in range(2)]
        r_at = [Res("at%d" % i) for i in range(2)]
        r_rt = [Res("rt%d" % i) for i in range(2)]
        r_xs = [Res("xs0"), Res("xs1")]
        r_ys = [Res("ys0"), Res("ys1")]
        r_hf = Res("hf")
        r_qt = Res("qt")
        r_kt = Res("kt")
        r_ot = Res("ot")
        r_vv = Res("vv")
        r_pt = [[Res("pt%d_%d" % (s, b)) for b in range(4)] for s in range(2)]
        r_rd = [Res("rd0"), Res("rd1")]
        SBK = [0, 1, 2, 7]
        SMAP = {(0, 0): (0, 0), (0, 1): (1, 0), (1, 0): (0, 256), (1, 1): (1, 256), (2, 0): (2, 0), (2, 1): (3, 0)}
        r_pool = {n: Res(n) for n in ["rstdp", "hp0", "hp1", "wa0", "wa1", "wb0", "wb1", "pl0", "pl1", "sv", "sqh"]}
        r_pw = Res("pw")
        r_setup = {n: Res(n) for n in ["cb0", "cb1", "cb2", "cb3", "raw", "tb", "msk"]}
        r_out = Res("out")
        arena_res = ([r_qt, r_kt, r_ot, r_vv, r_hf] + r_xs + r_ys + r_rd + r_pt[0] + r_pt[1]
                     + list(r_pool.values()) + list(r_setup.values()) + [r_pw])

        OT2 = AT[:, :, :, :].rearrange("p a m n -> p (a m n)")[:, 0:TMAX]
        OTS = [OT, OT2]
        WOB = RT[:, :, :].rearrange("p a n -> p (a n)").bitcast(BF16).rearrange("p (s n) -> p s n", s=2)
        r_ots = [[r_ot], [r_at[0], r_at[1]]]
        r_wob = [r_rt[0], r_rt[1]]
        state = dict(ring=0, sq=0, rs=0)

        def arena_phase():
            snap = []
            for r in arena_res:
                snap.extend(r.w.values())
                snap.extend(r.wd)
                snap.extend(r.r.values())
                snap.extend(r.rd)
            for e in ENGS:
                P.op(e, "nop", extra_deps=snap, nop=True)
            for r in arena_res:
                r.w = {}
                r.wd = []
                r.r = {}
                r.rd = []

        P.op("sp", "dma_start", kwargs=dict(out=GV[:], in_=gv_in[:, :, :]), writes=[r_const], dma_sem="c_misc")
        P.op("sp", "dma_start", kwargs=dict(out=IDF[:], in_=ident_in[:, :]), writes=[r_const], dma_sem="c_misc")
        P.op("sp", "dma_start", kwargs=dict(out=INVC[:], in_=invc_in[:, :, :]), writes=[r_const], dma_sem="c_misc")
        P.op("sp", "dma_start", kwargs=dict(out=JF, in_=jmat_in[:, :]), writes=[r_setup["msk"]], dma_sem="c_msk")
        P.op("sp", "dma_start", kwargs=dict(out=M01, in_=m01_in[:, :]), writes=[r_setup["msk"]], dma_sem="c_msk")
        P.op("sp", "dma_start", kwargs=dict(out=NEGM, in_=negm_in[:, :]), writes=[r_setup["msk"]], dma_sem="c_msk")
        P.op("pool", "memset", (ONESM[:], 1.0 / D), writes=[r_const])
        P.op("dve", "tensor_copy", (JB[:], JF), reads=[r_setup["msk"]], writes=[r_const])

        P.join("pe", [r_const])
        P.join("dve", [r_const])
        P.join("act", [r_const])
        early_load = [True]

        cast_jobs = []

        def add_cast(src2d, dst2d, ncols):
            nrows = src2d.shape[0]
            assert nrows % 128 == 0 and src2d.shape[1] == ncols
            for a in range(nrows // 128):
                cast_jobs.append((src2d[a * 128:(a + 1) * 128, :], dst2d[a * 128:(a + 1) * 128, :], ncols))

        add_cast(w_up.rearrange("l r c -> (l r) c"), wup_b.rearrange("l r c -> (l r) c"), 4096)
        add_cast(w_down.rearrange("l (r f) c -> (l r) (f c)", f=4),
                 wdn_b.rearrange("l (r f) c -> (l r) (f c)", f=4), 4096)
        add_cast(w_qkv.rearrange("l r c -> (l r) c"), wqkv_b.rearrange("l r c -> (l r) c"), 3072)
        add_cast(w_o.rearrange("l (r f) c -> (l r) (f c)", f=4),
                 wo_b.rearrange("l (r f) c -> (l r) (f c)", f=4), 4096)
        add_cast(pool_w.rearrange("l g (r f) c -> (l g r) (f c)", f=16),
                 pw_b.rearrange("l g (r f) c -> (l g r) (f c)", f=16), 4096)
        cb_res = [r_setup["cb0"], r_setup["cb1"], r_setup["cb2"], r_setup["cb3"]]
        CBH = HC[:, :, :].rearrange("p c n -> p (c n)")[:, 0:16384].rearrange("p (s n) -> p s n", s=4)

        def emit_cast(i):
            src, dst, ncols = cast_jobs[i]
            s_ = i % 4
            P.op("pool", "dma_start", kwargs=dict(out=CBH[:, s_, 0:ncols], in_=src),
                 fresh=[cb_res[s_]], dma_sem="cast%d" % s_)
            P.op("sp", "dma_start", kwargs=dict(out=dst, in_=CBH[:, s_, 0:ncols]),
                 reads=[cb_res[s_]], writes=[r_scr], dma_sem="cst%d" % s_)

        m01b = M01.unsqueeze(1).to_broadcast([128, 15, 64])
        negb = NEGM.unsqueeze(1).to_broadcast([128, 15, 64])

        def emit_table(l, h):
            base = RPB_PAD + h * 15 * 31 - 48
            src0 = bass.AP(rpbx.tensor, l * RPB_L + base, [[1, 64], [31, 15], [1, 64]])
            src1 = bass.AP(rpbx.tensor, l * RPB_L + base - 31, [[1, 64], [31, 15], [1, 64]])
            P.op("sp", "dma_start", kwargs=dict(out=RAW[0:64, :, :], in_=src0),
                 fresh=[r_setup["raw"]], dma_sem="tblraw")
            P.op("sp", "dma_start", kwargs=dict(out=RAW[64:128, :, :], in_=src1),
                 writes=[r_setup["raw"]], dma_sem="tblraw")
            P.op("dve", "tensor_tensor", (RAW[:, :, :], RAW[:, :, :], m01b, ALU.mult),
                 reads=[r_setup["raw"], r_setup["msk"]], writes=[r_setup["raw"]])
            negb5 = NEGM.unsqueeze(1).to_broadcast([128, 5, 64])
            negb7 = NEGM.unsqueeze(1).to_broadcast([128, 7, 64])
            negb3 = NEGM.unsqueeze(1).to_broadcast([128, 3, 64])
            P.op("dve", "tensor_tensor", (TB[:, 0:5, :], RAW[:, 0:5, :], negb5, ALU.add),
                 reads=[r_setup["raw"], r_setup["msk"]], fresh=[r_setup["tb"]])
            P.op("dve", "tensor_tensor", (TB[:, 6:13, :], RAW[:, 5:12, :], negb7, ALU.add),
                 reads=[r_setup["raw"], r_setup["msk"]], writes=[r_setup["tb"]])
            P.op("dve", "tensor_tensor", (TB[:, 14:17, :], RAW[:, 12:15, :], negb3, ALU.add),
                 reads=[r_setup["raw"], r_setup["msk"]], writes=[r_setup["tb"]])
            P.op("dve", "memset", (TB[64:128, 5, :], NEG), writes=[r_setup["tb"]])
            P.op("dve", "memset", (TB[0:64, 13, :], NEG), writes=[r_setup["tb"]])
            P.op("dve", "tensor_copy", (TB[0:64, 5, :], TB[0:64, 4, :]),
                 reads=[r_setup["tb"]], writes=[r_setup["tb"]])
            P.op("dve", "tensor_copy", (TB[64:128, 13, :], TB[64:128, 14, :]),
                 reads=[r_setup["tb"]], writes=[r_setup["tb"]])
            g, hh = h // 2, h % 2
            dstv = tbl_b[l, g, :, hh * NSLOT * 64:(hh + 1) * NSLOT * 64]
            P.op("sp", "dma_start", kwargs=dict(out=dstv, in_=TB[:, :, :].rearrange("p j n -> p (j n)")),
                 reads=[r_setup["tb"]], writes=[r_scr], dma_sem="tblst")

        tjobs = [(l, h) for l in range(2) for h in range(16)]
        ti = 0
        for i in range(len(cast_jobs)):
            emit_cast(i)
            if i % 2 == 1 and ti < len(tjobs):
                emit_table(*tjobs[ti])
                ti += 1
        while ti < len(tjobs):
            emit_table(*tjobs[ti])
            ti += 1
        P.join("sp", [r_scr])

        def ring_load(descr):
            s = state["ring"]
            state["ring"] ^= 1
            first = True
            for dstf, src in descr:
                kw = dict(fresh=[r_ring[s]]) if first else dict(writes=[r_ring[s]])
                P.op("sp", "dma_start", kwargs=dict(out=dstf(s), in_=src), reads=[r_scr], dma_sem="ring%d" % s, **kw)
                first = False
            return s

        def norm_tile(t, gi, out_kind):
            t0 = t * TT
            for c in range(NCH):
                s = state["sq"]
                state["sq"] = (s + 1) % 4
                P.op("act", "activation", (SQ[:, s, :], XT[:, c, t0:t0 + TT], AF.Square),
                     reads=[r_xt[c][t]], writes=[r_sq[s]])
                P.op("pe", "matmul", (PS[7][:, :], ONESM[:, :], SQ[:, s, :]),
                     dict(start=(c == 0), stop=(c == NCH - 1)), reads=[r_sq[s]], writes=[r_bank[7]])
            rs = state["rs"]
            state["rs"] ^= 1
            P.op("act", "activation", (RS[:, rs, :], PS[7][:, :], AF.Ln), dict(bias=EPS_AP),
                 reads=[r_bank[7]], writes=[r_rs[rs]])
            P.op("act", "activation", (RS[:, rs, :], RS[:, rs, :], AF.Exp), dict(scale=-0.5),
                 reads=[r_rs[rs]], writes=[r_rs[rs]])
            for c in range(NCH):
                if out_kind == "hc":
                    out = HC[:, c, t0:t0 + TT]
                    wr = [r_hc[t]]
                else:
                    out = HF[:, c, :]
                    wr = [r_hf]
                P.op("dve", "scalar_tensor_tensor",
                     (out, XT[:, c, t0:t0 + TT], GV[:, gi, c:c + 1], RS[:, rs, :], ALU.mult, ALU.mult),
                     reads=[r_xt[c][t], r_rs[rs]], writes=wr)

        def load_tile(u, t):
            x = xsrc[u["src"]]
            for sub in range(4 * t, 4 * t + 4):
                s = sub % 2
                tok = u["tok0"] + sub * 128
                P.op("sp", "dma_start", kwargs=dict(out=XS[:, s, :], in_=x[u["seq"], tok:tok + 128, :]),
                     fresh=[r_xs[s]], dma_sem="xs%d" % s)
                for half in range(2):
                    bk = 4 + (sub % 2) * 2 + half
                    for cc in range(4):
                        c = half * 4 + cc
                        P.op("pe", "transpose",
                             (PS[bk][:, cc * 128:(cc + 1) * 128], XS[:, s, c * 128:(c + 1) * 128], IDF[:, :]),
                             reads=[r_xs[s]], writes=[r_bank[bk]])
                    outv = XT[:, half * 4:half * 4 + 4, sub * 128:(sub + 1) * 128]
                    inv = PS[bk][:, :].rearrange("p (c n) -> p c n", c=4)
                    wr = [r_xt[half * 4 + cc][t] for cc in range(4)]
                    if half == 0:
                        P.op("act", "copy", (outv, inv), reads=[r_bank[bk]], writes=wr)
                    else:
                        P.op("dve", "tensor_copy", (outv, inv), reads=[r_bank[bk]], writes=wr)

        def phase_load(u):
            arena_phase()
            for t in range(u["R"] * 64 // TT):
                load_tile(u, t)

        def phase_mlp(li, tiles, pre_tile=None):
            gi = 4 + li
            ubank = [0]
            dbank = [0]

            def up(s, t, ab):
                t0 = t * TT
                for m in range(4):
                    bk = ubank[0] % 3
                    ubank[0] += 1
                    for k in range(NCH):
                        P.op("pe", "matmul",
                             (PS[bk][:, :], RING[:, s, k * FS + m * 128:k * FS + (m + 1) * 128], HC[:, k, t0:t0 + TT]),
                             dict(start=(k == 0), stop=(k == NCH - 1)),
                             reads=[r_ring[s], r_hc[t]], writes=[r_bank[bk]])
                    rt = m % 2
                    P.op("act", "activation", (RT[:, rt, :], PS[bk][:, :], AF.Relu),
                         reads=[r_bank[bk]], writes=[r_rt[rt]])
                    P.op("pool", "tensor_tensor", (AT[:, ab, m, :], RT[:, rt, :], RT[:, rt, :], ALU.mult),
                         reads=[r_rt[rt]], writes=[r_at[ab]])

            def down(s, t, ab):
                t0 = t * TT
                for fo in range(NCH):
                    bk = 3 + dbank[0] % 4
                    dbank[0] += 1
                    for m in range(4):
                        P.op("pe", "matmul",
                             (PS[bk][:, :], RING[:, s, 4096 + m * D + fo * 128:4096 + m * D + (fo + 1) * 128],
                              AT[:, ab, m, :]),
                             dict(start=(m == 0), stop=(m == 3)),
                             reads=[r_ring[s], r_at[ab]], writes=[r_bank[bk]])
                    P.op("dve", "tensor_tensor",
                         (XT[:, fo, t0:t0 + TT], XT[:, fo, t0:t0 + TT], PS[bk][:, :], ALU.add),
                         reads=[r_bank[bk], r_xt[fo][t]], writes=[r_xt[fo][t]])

            slots = {}
            pend = None
            i = 0
            for sl in range(NSL):
                for t in tiles:
                    if sl not in slots:
                        srcu = wup_b[li, :, sl * FS:(sl + 1) * FS].rearrange("(k p) n -> p k n", p=128)
                        srcd = wdn_b[li, sl * FS:(sl + 1) * FS, :].rearrange("(k p) n -> p k n", p=128)
                        slots[sl] = ring_load([
                            (lambda s: RING[:, s, 0:4096].rearrange("p (k n) -> p k n", k=NCH), srcu),
                            (lambda s: RING[:, s, 4096:8192].rearrange("p (k n) -> p k n", k=4), srcd)])
                    if sl == 0:
                        if pre_tile is not None:
                            ti = tiles.index(t)
                            if ti == 0:
                                pre_tile(t)
                            if ti + 1 < len(tiles):
                                pre_tile(tiles[ti + 1])
                        norm_tile(t, gi, "hc")
                    ab = i % 2
                    i += 1
                    up(slots[sl], t, ab)
                    if pend is not None:
                        down(*pend)
                    pend = (slots[sl], t, ab)
            down(*pend)

        def pool_begin(u, li):
            arena_phase()
            j = li // 2
            P.op("sp", "dma_start",
                 kwargs=dict(out=PW.rearrange("p (g k n) -> p g k n", g=4, k=2),
                             in_=pw_b[j].rearrange("g (k p) n -> p g k n", p=128)),
                 reads=[r_scr], fresh=[r_pw], dma_sem="pw")

        def pool_tile(u, li, t):
            j = li // 2
            gi = li
            psi = 8 + j
            T = u["R"] * 64
            ntile = T // TT
            t0 = t * TT
            has_right = (t < ntile - 1)
            nb = TT + (8 if has_right else 0)
            left_true = (u["off"] == 0) and t == 0
            right_true = (u["off"] + u["R"] == u["Rg"]) and t == ntile - 1
            L = TT + 16
            for c in range(NCH):
                sq = state["sq"]
                state["sq"] = (sq + 1) % 4
                P.op("act", "activation", (SQ[:, sq, :], XT[:, c, t0:t0 + TT], AF.Square),
                     reads=[r_xt[c][t]], writes=[r_sq[sq]])
                P.op("pe", "matmul", (PS[7][:, :], ONESM[:, :], SQ[:, sq, :]),
                     dict(start=(c == 0), stop=(c == NCH - 1)), reads=[r_sq[sq]], writes=[r_bank[7]])
            P.op("act", "activation", (RSTDP[:, 8:8 + TT], PS[7][:, :], AF.Ln), dict(bias=EPS_AP),
                 reads=[r_bank[7]], writes=[r_pool["rstdp"]])
            if has_right:
                P.op("act", "activation", (SQH[:, :, :], XT[:, :, t0 + TT:t0 + TT + 8], AF.Square),
                     reads=[r_xt[c][t + 1] for c in range(NCH)], writes=[r_pool["sqh"]])
                for c in range(NCH):
                    P.op("pe", "matmul", (PS[6][:, 0:8], ONESM[:, :], SQH[:, c, :]),
                         dict(start=(c == 0), stop=(c == NCH - 1)), reads=[r_pool["sqh"]], writes=[r_bank[6]])
                P.op("act", "activation", (RSTDP[:, 8 + TT:16 + TT], PS[6][:, 0:8], AF.Ln), dict(bias=EPS_AP),
                     reads=[r_bank[6]], writes=[r_pool["rstdp"]])
            P.op("act", "activation", (RSTDP[:, 8:8 + nb], RSTDP[:, 8:8 + nb], AF.Exp), dict(scale=-0.5),
                 reads=[r_pool["rstdp"]], writes=[r_pool["rstdp"]])
            for g in range(4):
                w = 2 ** (g + 1)
                add_eng = "dve" if g < 2 else "pool"
                pb = g % 2
                ch = []
                for ki in range(2):
                    c = 2 * g + ki
                    hb = ki
                    ch.append((ki, c, HP2[hb], WA2[hb], WB2[hb],
                               r_pool["hp%d" % hb], r_pool["wa%d" % hb], r_pool["wb%d" % hb]))
                for (ki, c, HPb, WAb, WBb, rhp, rwa, rwb) in ch:
                    if t == 0:
                        P.op("pool", "memset", (HPb[:, 0:8], 0.0), writes=[rhp])
                    else:
                        P.op("pool", "tensor_copy", (HPb[:, 0:8], SV[:, c, 0:8]), reads=[r_pool["sv"]], writes=[rhp])
                    if not has_right:
                        P.op("pool", "memset", (HPb[:, 8 + nb:L], 0.0), writes=[rhp])
                for (ki, c, HPb, WAb, WBb, rhp, rwa, rwb) in ch:
                    P.op("dve", "scalar_tensor_tensor",
                         (HPb[:, 8:8 + nb], XT[:, c, t0:t0 + nb], GV[:, gi, c:c + 1], RSTDP[:, 8:8 + nb],
                          ALU.mult, ALU.mult),
                         reads=[r_xt[c][t]] + ([r_xt[c][t + 1]] if has_right else []) + [r_pool["rstdp"]], writes=[rhp])
                if has_right:
                    for (ki, c, HPb, WAb, WBb, rhp, rwa, rwb) in ch:
                        P.op("pool", "tensor_copy", (SV[:, c, 0:8], HPb[:, TT:TT + 8]), reads=[rhp], writes=[r_pool["sv"]])
                for (ki, c, HPb, WAb, WBb, rhp, rwa, rwb) in ch:
                    P.op(add_eng, "tensor_tensor", (WAb[:, 1:L], HPb[:, 0:L - 1], HPb[:, 1:L], ALU.add),
                         reads=[rhp], writes=[rwa])
                if w >= 4:
                    for (ki, c, HPb, WAb, WBb, rhp, rwa, rwb) in ch:
                        P.op(add_eng, "tensor_tensor", (WBb[:, 2:L - 1], WAb[:, 1:L - 2], WAb[:, 3:L], ALU.add),
                             reads=[rwa], writes=[rwb])
                if w >= 8:
                    for (ki, c, HPb, WAb, WBb, rhp, rwa, rwb) in ch:
                        P.op(add_eng, "tensor_tensor", (WAb[:, 4:L - 3], WBb[:, 2:L - 5], WBb[:, 6:L - 1], ALU.add),
                             reads=[rwb], writes=[rwa])
                if w >= 16:
                    for (ki, c, HPb, WAb, WBb, rhp, rwa, rwb) in ch:
                        P.op(add_eng, "tensor_tensor", (WBb[:, 8:L - 8], WAb[:, 4:L - 12], WAb[:, 12:L - 4], ALU.add),
                             reads=[rwa], writes=[rwb])
                for (ki, c, HPb, WAb, WBb, rhp, rwa, rwb) in ch:
                    if w in (2, 8):
                        cur, curr, oth, othr = WAb, rwa, WBb, rwb
                    else:
                        cur, curr, oth, othr = WBb, rwb, WAb, rwa
                    P.op("dve", "scalar_tensor_tensor",
                         (PL2[:, pb, ki, :], cur[:, 8:8 + TT], 1.0 / w, HPb[:, 8:8 + TT], ALU.mult, ALU.subtract),
                         reads=[curr, rhp], writes=[r_pool["pl%d" % pb]])
                    if left_true:
                        P.op("dve", "tensor_tensor", (oth[:, 0:8], cur[:, 8:16], INVC[:, c, 0:8], ALU.mult),
                             reads=[curr], writes=[othr])
                        P.op("dve", "tensor_tensor", (PL2[:, pb, ki, 0:8], oth[:, 0:8], HPb[:, 8:16], ALU.subtract),
                             reads=[othr, rhp], writes=[r_pool["pl%d" % pb]])
                    if right_true:
                        P.op("dve", "tensor_tensor", (oth[:, 0:8], cur[:, TT:TT + 8], INVC[:, c, 8:16], ALU.mult),
                             reads=[curr], writes=[othr])
                        P.op("dve", "tensor_tensor",
                             (PL2[:, pb, ki, TT - 8:TT], oth[:, 0:8], HPb[:, TT:TT + 8], ALU.subtract),
                             reads=[othr, rhp], writes=[r_pool["pl%d" % pb]])
                for mi in range(2):
                    bk = [3, 4, 5, 6][(2 * g + mi) % 4]
                    for ki in range(2):
                        wo_ = (g * 2 + ki) * 256 + mi * 128
                        P.op("pe", "matmul", (PS[bk][:, :], PW[:, wo_:wo_ + 128], PL2[:, pb, ki, :]),
                             dict(start=(ki == 0), stop=(ki == 1)),
                             reads=[r_pw, r_pool["pl%d" % pb]], writes=[r_bank[bk]])
                    c = 2 * g + mi
                    P.op("dve", "scalar_tensor_tensor",
                         (XT[:, c, t0:t0 + TT], PS[bk][:, :], GV[:, psi, c:c + 1], XT[:, c, t0:t0 + TT],
                          ALU.mult, ALU.add),
                         reads=[r_bank[bk], r_xt[c][t]], writes=[r_xt[c][t]])


        def phase_attn(u, li, ctiles, qrows=None):
            arena_phase()
            j = li // 2
            gi = li
            R = u["R"]
            T = R * 64
            ntile = T // TT
            st = attn_struct(R, u["off"], u["Rg"])
            if qrows is not None:
                st = [[it for it in items if it[0] in qrows] for items in st]
            blk_last = {}
            for p, items in enumerate(st):
                for (rl, sec, jj) in items:
                    blk_last[rl // 8] = p
            P.op("pool", "memset", (VV[:, :, 64:128], 1.0), writes=[r_vv])
            def phase_c(ga, gb):
                cb = 0
                for t in ctiles:
                    t0 = t * TT
                    for fo in range(NCH):
                        bk = [0, 1, 2, 7][cb % 4]
                        cb += 1
                        for qi, gq in enumerate((ga, gb)):
                            P.op("pe", "matmul",
                                 (PS[bk][:, :], WOB[:, gq % 2, fo * 128:(fo + 1) * 128], OTS[gq % 2][:, t0:t0 + TT]),
                                 dict(start=(qi == 0), stop=(qi == 1)),
                                 reads=[r_wob[gq % 2]] + r_ots[gq % 2], writes=[r_bank[bk]])
                        P.op("dve", "tensor_tensor",
                             (XT[:, fo, t0:t0 + TT], XT[:, fo, t0:t0 + TT], PS[bk][:, :], ALU.add),
                             reads=[r_bank[bk], r_xt[fo][t]], writes=[r_xt[fo][t]])

            pending_c = []
            for g in range(ngroups):
                s = ring_load([
                    (lambda s: RING[:, s, 0:1024].rearrange("p (k n) -> p k n", k=NCH),
                     wqkv_b[j, :, g * 128:(g + 1) * 128].rearrange("(k p) n -> p k n", p=128)),
                    (lambda s: RING[:, s, 1024:2048].rearrange("p (k n) -> p k n", k=NCH),
                     wqkv_b[j, :, D + g * 128:D + (g + 1) * 128].rearrange("(k p) n -> p k n", p=128)),
                    (lambda s: RING[:, s, 2048:3072].rearrange("p (k n) -> p k n", k=NCH),
                     wqkv_b[j, :, 2 * D + g * 128:2 * D + (g + 1) * 128].rearrange("(k p) n -> p k n", p=128)),
                    (lambda s: RING[:, s, 4096:4096 + TBLW], tbl_b[j, g, :, :]),
                ])
                for t in range(ntile):
                    t0 = t * TT
                    if g == 0:
                        norm_tile(t, gi, "hc")
                    for bk, wo_ in ((0, 0), (1, 1024)):
                        for k in range(NCH):
                            P.op("pe", "matmul",
                                 (PS[bk][:, :], RING[:, s, wo_ + k * 128:wo_ + (k + 1) * 128], HC[:, k, t0:t0 + TT]),
                                 dict(start=(k == 0), stop=(k == NCH - 1)),
                                 reads=[r_ring[s], r_hc[t]], writes=[r_bank[bk]])
                    P.op("act", "mul", (QT[:, t0:t0 + TT], PS[0][:, :], 0.125), reads=[r_bank[0]], writes=[r_qt])
                    P.op("dve", "tensor_copy", (KT[:, t0:t0 + TT], PS[1][:, :]), reads=[r_bank[1]], writes=[r_kt])
                    for sub in range(4):
                        for k in range(NCH):
                            P.op("pe", "matmul",
                                 (PS[2][:, sub * 128:(sub + 1) * 128], HC[:, k, t0 + sub * 128:t0 + (sub + 1) * 128],
                                  RING[:, s, 2048 + k * 128:2048 + (k + 1) * 128]),
                                 dict(start=(k == 0), stop=(k == NCH - 1)),
                                 reads=[r_ring[s], r_hc[t]], writes=[r_bank[2]])
                    vin = PS[2][:, :].rearrange("p (a n) -> p a n", a=4)
                    vout = VV[:, 4 * t:4 * t + 4, :].rearrange("p a (h n) -> p a h n", h=3)[:, :, 0:3:2, :]
                    vin4 = vin.rearrange("p a (h n) -> p a h n", h=2)
                    P.op("act", "copy", (vout, vin4), reads=[r_bank[2]], writes=[r_vv])

                if pending_c:
                    phase_c(*pending_c.pop())
                P.op("sp", "dma_start", kwargs=dict(out=WOB[:, g % 2, :], in_=wo_b[j, g * 128:(g + 1) * 128, :]),
                     reads=[r_scr], fresh=[r_wob[g % 2]], dma_sem="wob%d" % (g % 2))
                fresh_slot = set()

                def chunks(p):
                    rows = st[p]
                    out = []
                    if not rows:
                        return out
                    if rows[0:8]:
                        out.append((0, 1, 0, rows[0:8]))
                    if rows[8:]:
                        assert len(rows) <= 12
                        out.append((2, 3, 8, rows[8:]))
                    return out

                def qk(p):
                    ps_ = p % 2
                    chs = chunks(p)
                    for (sia, sib, l0, its) in chs:
                        ra = its[0][0]
                        nr = len(its)
                        for hb, si in ((0, sia), (1, sib)):
                            bk = SBK[si]
                            P.op("pe", "matmul",
                                 (PS[bk][:, 0:nr * 64],
                                  KT[hb * 64:(hb + 1) * 64, p * 128:(p + 1) * 128],
                                  QT[hb * 64:(hb + 1) * 64, ra * 64:(ra + nr) * 64]),
                                 dict(start=True, stop=False, skip_group_check=True),
                                 reads=[r_kt, r_qt], writes=[r_bank[bk]])
                    for (sia, sib, l0, its) in chs:
                        ra = its[0][0]
                        runs = []
                        for (rl, sec, jj) in its:
                            sl_ = SLOTS[(sec, jj)]
                            if runs and runs[-1][1] + runs[-1][2] == sl_ and runs[-1][0] + runs[-1][2] == rl:
                                runs[-1][2] += 1
                            else:
                                runs.append([rl, sl_, 1])
                        for hb, si in ((0, sia), (1, sib)):
                            bk = SBK[si]
                            for (rl0, sl0, n) in runs:
                                co = (rl0 - ra) * 64
                                to = 4096 + hb * NSLOT * 64 + sl0 * 64
                                P.op("pe", "matmul",
                                     (PS[bk][:, co:co + n * 64], JB[:, :], RING[:, s, to:to + n * 64]),
                                     dict(start=False, stop=True, skip_group_check=True),
                                     reads=[r_ring[s]], writes=[r_bank[bk]])
                    for (sia, sib, l0, its) in chs:
                        wd = len(its) * 64
                        for si in (sia, sib):
                            bk = SBK[si]
                            P.op("act", "activation", (PT[:, ps_, si, 0:wd], PS[bk][:, 0:wd], AF.Exp),
                                 reads=[r_bank[bk]], writes=[r_pt[ps_][si]])

                def pv(p):
                    ps_ = p % 2
                    for (sia, sib, l0, its) in chunks(p):
                        ra0 = its[0][0]
                        segs = []
                        for (rl, sec, jj) in its:
                            if segs and segs[-1][0] // 8 == rl // 8:
                                segs[-1][1] += 1
                            else:
                                segs.append([rl, 1])
                        for (ra, nr) in segs:
                            blk = ra // 8
                            slot = blk % 2
                            co = (ra % 8) * 64
                            po_ = (ra - ra0) * 64
                            for hb, si in ((0, sia), (1, sib)):
                                bk = (3 + slot) if hb == 0 else (5 + slot)
                                fr = (blk, hb) not in fresh_slot
                                fresh_slot.add((blk, hb))
                                P.op("pe", "matmul",
                                     (PS[bk][:, co:co + nr * 64], VV[:, p, hb * 64:hb * 64 + 128],
                                      PT[:, ps_, si, po_:po_ + nr * 64]),
                                     dict(start=fr, stop=False, skip_group_check=True),
                                     reads=[r_vv, r_pt[ps_][si]], writes=[r_bank[bk]])
                    for blk, lastp in blk_last.items():
                        if lastp != p:
                            continue
                        slot = blk % 2
                        c0 = blk * TT
                        bx, by = 3 + slot, 5 + slot
                        P.op("act", "activation", (RD[0:64, :], PS[bx][64:128, :], AF.Ln),
                             reads=[r_bank[bx]], writes=[r_rd[0]])
                        P.op("act", "activation", (RD[0:64, :], RD[0:64, :], AF.Exp), dict(scale=-1.0),
                             reads=[r_rd[0]], writes=[r_rd[0]])
                        P.op("dve", "tensor_tensor", (OTS[g % 2][0:64, c0:c0 + TT], PS[bx][0:64, :], RD[0:64, :], ALU.mult),
                             reads=[r_bank[bx], r_rd[0]], writes=r_ots[g % 2])
                        P.op("act", "activation", (RD[64:128, :], PS[by][0:64, :], AF.Ln),
                             reads=[r_bank[by]], writes=[r_rd[1]])
                        P.op("act", "activation", (RD[64:128, :], RD[64:128, :], AF.Exp), dict(scale=-1.0),
                             reads=[r_rd[1]], writes=[r_rd[1]])
                        P.op("dve", "tensor_tensor",
                             (OTS[g % 2][64:128, c0:c0 + TT], PS[by][64:128, :], RD[64:128, :], ALU.mult),
                             reads=[r_bank[by], r_rd[1]], writes=r_ots[g % 2])

                npair = len(st)
                if attdbg >= 2:
                    for p in range(npair):
                        qk(p)
                        if p > 0:
                            pv(p - 1)
                    pv(npair - 1)
                elif attdbg == 1:
                    for p in range(npair):
                        qk(p)
                if attdbg < 3:
                    continue

                if g % 2 == 1:
                    pending_c.append((g - 1, g))
            if pending_c:
                phase_c(*pending_c.pop())

        def final_tile(u, t, kst):
            y = ydst[u["dst"]]
            norm_tile(t, 10, "hf")
            for sub in range(4):
                ys = kst[0] % 2
                kst[0] += 1
                for half in range(2):
                    bk = half * 2 + (sub % 2)
                    for cc in range(4):
                        c = half * 4 + cc
                        P.op("pe", "transpose",
                             (PS[bk][:, cc * 128:(cc + 1) * 128], HF[:, c, sub * 128:(sub + 1) * 128], IDF[:, :]),
                             reads=[r_hf], writes=[r_bank[bk]])
                    if half == 0:
                        P.op("act", "copy", (YS[:, ys, 0:512], PS[bk][:, :]), reads=[r_bank[bk]], writes=[r_ys[ys]])
                    else:
                        P.op("dve", "tensor_copy", (YS[:, ys, 512:1024], PS[bk][:, :]),
                             reads=[r_bank[bk]], writes=[r_ys[ys]])
                tok = u["tok0"] + t * TT + sub * 128
                P.op("pool", "dma_start", kwargs=dict(out=y[u["seq"], tok:tok + 128, :], in_=YS[:, ys, :]),
                     reads=[r_ys[ys]], writes=[r_out], dma_sem="yst%d" % ys)

        def phase_final(u, tiles, unext=None):
            arena_phase()
            kst = [0]
            nt_next = (unext["R"] * 64 // TT) if unext is not None else 0
            loaded = set()
            for t in range(nt_next):
                if t not in tiles:
                    load_tile(unext, t)
                    loaded.add(t)
            for t in tiles:
                final_tile(u, t, kst)
                if t < nt_next and t not in loaded:
                    load_tile(unext, t)
                    loaded.add(t)
            for t in range(nt_next):
                if t not in loaded:
                    load_tile(unext, t)

        plist = []
        for li in range(DEPTH):
            plist.append(("pool" if li % 2 == 0 else "attn", li))
            plist.append(("mlp", li))
        plist = plist[:nphases]
        for u in units:
            ntile = u["R"] * 64 // TT
            all_tiles = list(range(ntile))
            out_tiles = list(range(u["out_lo"] // 8, u["out_lo"] // 8 + 4))
            if u is units[0]:
                phase_load(u)
            for kind, li in plist:
                last = (li == DEPTH - 1)
                if kind == "pool":
                    pool_begin(u, li)
                    if ("mlp", li) not in plist:
                        for t in all_tiles:
                            pool_tile(u, li, t)
                elif kind == "attn":
                    qr = set(range(u["out_lo"], u["out_lo"] + 32)) if (last and u["R"] > 32) else None
                    phase_attn(u, li, out_tiles if last else all_tiles, qrows=qr)
                else:
                    pre = None
                    if li % 2 == 0:
                        pre = (lambda t, u=u, li=li: pool_tile(u, li, t))
                    phase_mlp(li, out_tiles if last else all_tiles, pre_tile=pre)
            ui = units.index(u)
            phase_final(u, out_tiles, units[ui + 1] if ui + 1 < len(units) else None)
        P.join("sp", [r_out])
        P.emit(block, sems, dma_sems)
    return nc


_CACHE = {}


def kernel(x_prompt, x_sample, norm_mix, pool_w, pool_scale, w_qkv, rpb, w_o, norm_mlp, w_up, w_down, norm_final):
    if "nc" not in _CACHE:
        _CACHE["nc"] = build_program()
    nc = _CACHE["nc"]
    f32 = np.float32
    hc = host_consts()
    gv = np.concatenate([np.asarray(norm_mix, f32), np.asarray(norm_mlp, f32), np.asarray(pool_scale, f32),
                         np.asarray(norm_final, f32)[None, :]], axis=0)
    gv = np.ascontiguousarray(gv.reshape(11, 8, 128).transpose(2, 0, 1))
    rp = np.asarray(rpb, f32)[:, :, ::-1, ::-1].reshape(2, -1)
    rpbx = np.zeros((2, RPB_L), f32)
    rpbx[:, RPB_PAD:RPB_PAD + rp.shape[1]] = rp
    shared = dict(w_up=np.ascontiguousarray(w_up, dtype=f32), w_down=np.ascontiguousarray(w_down, dtype=f32),
                  w_qkv=np.ascontiguousarray(w_qkv, dtype=f32), w_o=np.ascontiguousarray(w_o, dtype=f32),
                  pool_w=np.ascontiguousarray(pool_w, dtype=f32), gv=gv, rpbx=rpbx,
                  ident=hc["ident"], jmat=hc["jmat"], m01=hc["m01"], negm=hc["negm"], invc=hc["invc"])
    xp = np.asarray(x_prompt, f32)
    xs = np.asarray(x_sample, f32)
    in_maps = []
    for i in range(NCORES):
        m = dict(shared)
        m["xp"] = np.ascontiguousarray(xp[4 * i:4 * i + 4])
        m["xs"] = np.ascontiguousarray(xs[2 * i:2 * i + 2])
        in_maps.append(m)
    res = run_bass_kernel_spmd(nc, in_maps, core_ids=list(range(NCORES)))
    yp = np.concatenate([np.asarray(r["yp"], f32) for r in res.results], axis=0)
    ys = np.concatenate([np.asarray(r["ys"], f32) for r in res.results], axis=0)
    return (yp, ys)
```

```python
import contextlib
import numpy as np
import concourse.bass as bass
import concourse.mybir as mybir
from concourse.bass_utils import run_bass_kernel_spmd

F32 = mybir.dt.float32
BF16 = mybir.dt.bfloat16
ALU = mybir.AluOpType
AF = mybir.ActivationFunctionType

NCORES = 8
D = 1024
NCH = 8
FF = 4096
TT = 512
DEPTH = 4
EPS = 1e-6
NEG = -1e30
TMAX = 2560
FS = 512
NSL = FF // FS
RPB_PAD = 128
RPB_L = 16 * 15 * 31 + 2 * RPB_PAD

ENGS = ("pe", "act", "dve", "pool", "sp")


class Op:
    __slots__ = ("eng", "fn", "deps", "marked", "count", "dma_sem", "dma_val", "is_nop")

    def __init__(self, eng, fn):
        self.eng = eng
        self.fn = fn
        self.deps = []
        self.marked = False
        self.count = None
        self.dma_sem = None
        self.dma_val = None
        self.is_nop = False


class Res:
    __slots__ = ("name", "w", "wd", "r", "rd")

    def __init__(self, name):
        self.name = name
        self.w = {}
        self.wd = []
        self.r = {}
        self.rd = []


class Prog:
    def __init__(self):
        self.q = {e: [] for e in ENGS}
        self.dma_counts = {}

    def op(self, eng, meth, args=(), kwargs=None, reads=(), writes=(), fresh=(), dma_sem=None, extra_deps=(),
           nop=False):
        o = Op(eng, (meth, tuple(args), kwargs or {}))
        o.is_nop = nop
        raw = []
        oth = []
        for r in reads:
            raw.extend(r.w.values())
            raw.extend(r.wd)
        for w in tuple(writes) + tuple(fresh):
            oth.extend(w.w.values())
            oth.extend(w.wd)
            oth.extend(w.r.values())
            oth.extend(w.rd)
        raw.extend(extra_deps)
        seen = set()
        for lst, is_raw in ((raw, True), (oth, False)):
            for d in lst:
                if d is None or id(d) in seen:
                    continue
                seen.add(id(d))
                if d.eng == eng and d.dma_sem is None:
                    if d.is_nop or eng == "pe" or eng == "sp":
                        continue
                    if not is_raw and dma_sem is None:
                        continue
                o.deps.append(d)
        for r in reads:
            if dma_sem is not None:
                r.rd.append(o)
            else:
                r.r[eng] = o
        for w in fresh:
            w.w = {}
            w.wd = []
            w.r = {}
            w.rd = []
        for w in tuple(writes) + tuple(fresh):
            if dma_sem is not None:
                w.wd.append(o)
            else:
                w.w[eng] = o
        if dma_sem is not None:
            o.dma_sem = dma_sem
            v = self.dma_counts.get(dma_sem, 0) + 16
            self.dma_counts[dma_sem] = v
            o.dma_val = v
        self.q[eng].append(o)
        return o

    def join(self, eng, res_list):
        o = self.op(eng, "nop", reads=res_list, nop=True)
        for r in res_list:
            r.w = {eng: o}
            r.wd = []
        return o

    def finalize(self):
        for e in ENGS:
            for o in self.q[e]:
                for d in o.deps:
                    if d.dma_sem is None:
                        d.marked = True
        for e in ENGS:
            c = 0
            for o in self.q[e]:
                if o.marked:
                    c += 1
                    o.count = c

    def emit(self, block, sems, dma_sems):
        self.finalize()
        prog = self

        def run(engname, engine):
            waited = {}
            for o in prog.q[engname]:
                for d in o.deps:
                    if d.dma_sem is not None:
                        key = ("dma", d.dma_sem)
                        val = d.dma_val
                        sem = dma_sems[d.dma_sem]
                    else:
                        key = d.eng
                        val = d.count
                        sem = sems[d.eng]
                    if waited.get(key, 0) >= val:
                        continue
                    waited[key] = val
                    engine.wait_ge(sem, val)
                meth, args, kwargs = o.fn
                ins = getattr(engine, meth)(*args, **kwargs)
                if o.dma_sem is not None:
                    ins.then_inc(dma_sems[o.dma_sem], 16)
                elif o.marked:
                    ins.then_inc(sems[engname], 1)

        @block.tensor
        def _(t):
            run("pe", t)

        @block.scalar
        def _(s):
            run("act", s)

        @block.vector
        def _(v):
            run("dve", v)

        @block.gpsimd
        def _(g):
            run("pool", g)

        @block.sync
        def _(s):
            run("sp", s)


def r0_local(rl, R, off, Rg):
    r0g = min(max(rl + off - 4, 0), Rg - 8)
    return min(max(r0g - off, 0), R - 8)


def attn_struct(R, off, Rg):
    pairs = []
    for p in range(R // 2):
        items = []
        for rl in range(R):
            r0 = r0_local(rl, R, off, Rg)
            k0, k1 = 2 * p, 2 * p + 1
            v0 = r0 <= k0 <= r0 + 7
            v1 = r0 <= k1 <= r0 + 7
            if not (v0 or v1):
                continue
            j = rl - k0 + 7
            assert 0 <= j <= 14
            sec = "11" if (v0 and v1) else ("10" if v0 else "01")
            items.append((rl, sec, j))
        rows = [it[0] for it in items]
        assert rows == list(range(rows[0], rows[-1] + 1))
        pairs.append(items)
    return pairs


SLOTS = {}
for _j in range(15):
    SLOTS[("11", _j)] = _j if _j <= 4 else (_j + 1 if _j <= 11 else _j + 2)
SLOTS[("10", 4)] = 5
SLOTS[("01", 12)] = 13
NSLOT = 17
TBLW = 2 * NSLOT * 64


def make_units():
    units = []
    for i in range(4):
        units.append(dict(src="xp", dst="yp", seq=i, tok0=0, R=32, off=0, Rg=32, out_lo=0))
    for i in range(2):
        units.append(dict(src="xs", dst="ys", seq=i, tok0=0, R=40, off=0, Rg=64, out_lo=0))
        units.append(dict(src="xs", dst="ys", seq=i, tok0=24 * 64, R=40, off=24, Rg=64, out_lo=8))
    return units


def host_consts():
    ident = np.eye(128, dtype=np.float32)
    jm = np.zeros((128, 128), np.float32)
    for k in range(128):
        jm[k, (k // 64) * 64 + 63 - (k % 64)] = 1.0
    m01 = np.zeros((128, 64), np.float32)
    for p in range(128):
        kc = 63 - (p % 64)
        for c in range(64):
            ws = min(max(c - 8, 0), 48)
            if ws <= kc < ws + 16:
                m01[p, c] = 1.0
    negm = ((1.0 - m01) * np.float32(NEG)).astype(np.float32)
    invc = np.zeros((128, 8, 16), np.float32)
    for c in range(8):
        w = 2 ** (c // 2 + 1)
        for j in range(8):
            invc[:, c, j] = 1.0 / min(w, j + w // 2)
            invc[:, c, 8 + j] = 1.0 / min(w, (8 - j) + w // 2)
    return dict(ident=ident, jmat=jm, m01=m01, negm=negm, invc=invc)


def build_program(units=None, nphases=2 * DEPTH, attdbg=9, ngroups=8):
    nc = bass.Bass("TRN2", target_bir_lowering=False)
    P = Prog()
    EPS_AP = EPS
    units = make_units() if units is None else units

    def din(name, shape):
        return nc.dram_tensor(name, list(shape), F32, kind="ExternalInput").ap()

    xsrc = {"xp": din("xp", [4, 2048, D]), "xs": din("xs", [2, 4096, D])}
    ydst = {"yp": nc.dram_tensor("yp", [4, 2048, D], F32, kind="ExternalOutput").ap(),
            "ys": nc.dram_tensor("ys", [2, 4096, D], F32, kind="ExternalOutput").ap()}
    w_up = din("w_up", [DEPTH, D, FF])
    w_down = din("w_down", [DEPTH, FF, D])
    w_qkv = din("w_qkv", [2, D, 3 * D])
    w_o = din("w_o", [2, D, D])
    pool_w = din("pool_w", [2, 4, 256, 256])
    gv_in = din("gv", [128, 11, 8])
    rpbx = din("rpbx", [2, RPB_L])
    ident_in = din("ident", [128, 128])
    jmat_in = din("jmat", [128, 128])
    m01_in = din("m01", [128, 64])
    negm_in = din("negm", [128, 64])
    invc_in = din("invc", [128, 8, 16])

    wup_b = nc.dram_tensor("wup_b", [DEPTH, D, FF], BF16).ap()
    wdn_b = nc.dram_tensor("wdn_b", [DEPTH, FF, D], BF16).ap()
    wqkv_b = nc.dram_tensor("wqkv_b", [2, D, 3 * D], BF16).ap()
    wo_b = nc.dram_tensor("wo_b", [2, D, D], BF16).ap()
    pw_b = nc.dram_tensor("pw_b", [2, 4, 256, 256], BF16).ap()
    tbl_b = nc.dram_tensor("tbl_b", [2, 8, 128, TBLW], BF16).ap()

    with contextlib.ExitStack() as es:
        def sb(name, shape, dt):
            return es.enter_context(nc.sbuf_tensor(name, list(shape), dt))

        XT = sb("XT", [128, NCH, TMAX], F32)
        HC = sb("HC", [128, NCH, TMAX], BF16)
        RING = sb("RING", [128, 2, 8192], BF16)
        GV = sb("GV", [128, 11, 8], F32)
        ONESM = sb("ONESM", [128, 128], BF16)
        IDF = sb("IDF", [128, 128], F32)
        JB = sb("JB", [128, 128], BF16)
        INVC = sb("INVC", [128, 8, 16], F32)
        SQ = sb("SQ", [128, 4, TT], BF16)
        RS = sb("RS", [128, 2, TT], F32)
        AT = sb("AT", [128, 2, 4, TT], BF16)
        RT = sb("RT", [128, 2, TT], F32)
        A0N = 16640
        A0 = sb("A0", [128, A0N], BF16)
        PS = [es.enter_context(nc.psum_tensor("PS%d" % i, [128, 512], F32)) for i in range(8)]

        sems = {e: es.enter_context(nc.semaphore("s_" + e)) for e in ENGS}
        dma_names = ["c_misc", "c_msk", "cast0", "cast1", "cast2", "cast3", "cst0", "cst1", "cst2", "cst3", "tblraw", "tblst", "ring0", "ring1",
                     "xs0", "xs1", "yst0", "yst1", "pw", "wob0", "wob1"]
        dma_sems = {n: es.enter_context(nc.semaphore("d_" + n)) for n in dma_names}
        block = es.enter_context(nc.Block())

        def a0_bf(off, n):
            assert off + n <= A0N
            return A0[:, off:off + n]

        def a0_f32(off, n):
            assert off % 2 == 0 and off + 2 * n <= A0N
            return A0[:, off:off + 2 * n].bitcast(F32)

        QT = a0_bf(0, TMAX)
        KT = a0_bf(2560, TMAX)
        OT = a0_bf(5120, TMAX)
        VV = a0_bf(7680, 20 * 192).rearrange("p (a b) -> p a b", b=192)
        PT = a0_bf(11520, 2 * 4 * 512).rearrange("p (s b n) -> p s b n", s=2, b=4)
        RD = a0_f32(15616, 512)
        YS = a0_f32(0, 2048).rearrange("p (s n) -> p s n", s=2)
        XS = a0_f32(12288, 2048).rearrange("p (s n) -> p s n", s=2)
        HF = a0_f32(4096, NCH * TT).rearrange("p (c n) -> p c n", c=NCH)
        PB2 = TT + 16
        RSTDP = a0_f32(0, PB2)
        HP2 = [a0_f32(2 * PB2 * (1 + i), PB2) for i in range(2)]
        WA2 = [a0_f32(2 * PB2 * (3 + i), PB2) for i in range(2)]
        WB2 = [a0_f32(2 * PB2 * (5 + i), PB2) for i in range(2)]
        po = 2 * PB2 * 7
        PL2 = a0_bf(po, 2 * 2 * TT).rearrange("p (b k n) -> p b k n", b=2, k=2)
        SV = a0_f32(po + 2048, 64).rearrange("p (c n) -> p c n", c=NCH)
        SQH = a0_bf(po + 2048 + 128, 64).rearrange("p (c n) -> p c n", c=NCH)
        PW = a0_bf(po + 2048 + 128 + 64, 2048)
        CB = a0_bf(0, 8192).rearrange("p (s n) -> p s n", s=2)
        RAW = a0_f32(8192, 15 * 64).rearrange("p (j n) -> p j n", j=15)
        TB = a0_bf(8192 + 1920, NSLOT * 64).rearrange("p (j n) -> p j n", j=NSLOT)
        so = 8192 + 1920 + NSLOT * 64
        M01 = a0_f32(so, 64)
        NEGM = a0_f32(so + 128, 64)
        JF = a0_f32(so + 256, 128)

        r_bank = [Res("bank%d" % i) for i in range(8)]
        r_xt = [[Res("xt%d_%d" % (c, t)) for t in range(TMAX // TT)] for c in range(NCH)]
        r_hc = [Res("hc%d" % t) for t in range(TMAX // TT)]
        r_ring = [Res("ring0"), Res("ring1")]
        r_const = Res("const")
        r_scr = Res("scr")
        r_sq = [Res("sq%d" % i) for i in range(4)]
        r_rs = [Res("rs%d" % i) for i in range(2)]
        r_at = [Res("at%d" % i) for i in range(2)]
        r_rt = [Res("rt%d" % i) for i in range(2)]
        r_xs = [Res("xs0"), Res("xs1")]
        r_ys = [Res("ys0"), Res("ys1")]
        r_hf = Res("hf")
        r_qt = Res("qt")
        r_kt = Res("kt")
        r_ot = Res("ot")
        r_vv = Res("vv")
        r_pt = [[Res("pt%d_%d" % (s, b)) for b in range(4)] for s in range(2)]
        r_rd = [Res("rd0"), Res("rd1")]
        SBK = [0, 1, 2, 7]
        SMAP = {(0, 0): (0, 0), (0, 1): (1, 0), (1, 0): (0, 256), (1, 1): (1, 256), (2, 0): (2, 0), (2, 1): (3, 0)}
        r_pool = {n: Res(n) for n in ["rstdp", "hp0", "hp1", "wa0", "wa1", "wb0", "wb1", "pl0", "pl1", "sv", "sqh"]}
        r_pw = Res("pw")
        r_setup = {n: Res(n) for n in ["cb0", "cb1", "cb2", "cb3", "raw", "tb", "msk"]}
        r_out = Res("out")
        arena_res = ([r_qt, r_kt, r_ot, r_vv, r_hf] + r_xs + r_ys + r_rd + r_pt[0] + r_pt[1]
                     + list(r_pool.values()) + list(r_setup.values()) + [r_pw])

        OT2 = AT[:, :, :, :].rearrange("p a m n -> p (a m n)")[:, 0:TMAX]
        OTS = [OT, OT2]
        WOB = RT[:, :, :].rearrange("p a n -> p (a n)").bitcast(BF16).rearrange("p (s n) -> p s n", s=2)
        r_ots = [[r_ot], [r_at[0], r_at[1]]]
        r_wob = [r_rt[0], r_rt[1]]
        state = dict(ring=0, sq=0, rs=0)

        def arena_phase():
            snap = []
            for r in arena_res:
                snap.extend(r.w.values())
                snap.extend(r.wd)
                snap.extend(r.r.values())
                snap.extend(r.rd)
            for e in ENGS:
                P.op(e, "nop", extra_deps=snap, nop=True)
            for r in arena_res:
                r.w = {}
                r.wd = []
                r.r = {}
                r.rd = []

        P.op("sp", "dma_start", kwargs=dict(out=GV[:], in_=gv_in[:, :, :]), writes=[r_const], dma_sem="c_misc")
        P.op("sp", "dma_start", kwargs=dict(out=IDF[:], in_=ident_in[:, :]), writes=[r_const], dma_sem="c_misc")
        P.op("sp", "dma_start", kwargs=dict(out=INVC[:], in_=invc_in[:, :, :]), writes=[r_const], dma_sem="c_misc")
        P.op("sp", "dma_start", kwargs=dict(out=JF, in_=jmat_in[:, :]), writes=[r_setup["msk"]], dma_sem="c_msk")
        P.op("sp", "dma_start", kwargs=dict(out=M01, in_=m01_in[:, :]), writes=[r_setup["msk"]], dma_sem="c_msk")
        P.op("sp", "dma_start", kwargs=dict(out=NEGM, in_=negm_in[:, :]), writes=[r_setup["msk"]], dma_sem="c_msk")
        P.op("pool", "memset", (ONESM[:], 1.0 / D), writes=[r_const])
        P.op("dve", "tensor_copy", (JB[:], JF), reads=[r_setup["msk"]], writes=[r_const])

        P.join("pe", [r_const])
        P.join("dve", [r_const])
        P.join("act", [r_const])
        early_load = [True]

        cast_jobs = []

        def add_cast(src2d, dst2d, ncols):
            nrows = src2d.shape[0]
            assert nrows % 128 == 0 and src2d.shape[1] == ncols
            for a in range(nrows // 128):
                cast_jobs.append((src2d[a * 128:(a + 1) * 128, :], dst2d[a * 128:(a + 1) * 128, :], ncols))

        add_cast(w_up.rearrange("l r c -> (l r) c"), wup_b.rearrange("l r c -> (l r) c"), 4096)
        add_cast(w_down.rearrange("l (r f) c -> (l r) (f c)", f=4),
                 wdn_b.rearrange("l (r f) c -> (l r) (f c)", f=4), 4096)
        add_cast(w_qkv.rearrange("l r c -> (l r) c"), wqkv_b.rearrange("l r c -> (l r) c"), 3072)
        add_cast(w_o.rearrange("l (r f) c -> (l r) (f c)", f=4),
                 wo_b.rearrange("l (r f) c -> (l r) (f c)", f=4), 4096)
        add_cast(pool_w.rearrange("l g (r f) c -> (l g r) (f c)", f=16),
                 pw_b.rearrange("l g (r f) c -> (l g r) (f c)", f=16), 4096)
        cb_res = [r_setup["cb0"], r_setup["cb1"], r_setup["cb2"], r_setup["cb3"]]
        CBH = HC[:, :, :].rearrange("p c n -> p (c n)")[:, 0:16384].rearrange("p (s n) -> p s n", s=4)

        def emit_cast(i):
            src, dst, ncols = cast_jobs[i]
            s_ = i % 4
            P.op("pool", "dma_start", kwargs=dict(out=CBH[:, s_, 0:ncols], in_=src),
                 fresh=[cb_res[s_]], dma_sem="cast%d" % s_)
            P.op("sp", "dma_start", kwargs=dict(out=dst, in_=CBH[:, s_, 0:ncols]),
                 reads=[cb_res[s_]], writes=[r_scr], dma_sem="cst%d" % s_)

        m01b = M01.unsqueeze(1).to_broadcast([128, 15, 64])
        negb = NEGM.unsqueeze(1).to_broadcast([128, 15, 64])

        def emit_table(l, h):
            base = RPB_PAD + h * 15 * 31 - 48
            src0 = bass.AP(rpbx.tensor, l * RPB_L + base, [[1, 64], [31, 15], [1, 64]])
            src1 = bass.AP(rpbx.tensor, l * RPB_L + base - 31, [[1, 64], [31, 15], [1, 64]])
            P.op("sp", "dma_start", kwargs=dict(out=RAW[0:64, :, :], in_=src0),
                 fresh=[r_setup["raw"]], dma_sem="tblraw")
            P.op("sp", "dma_start", kwargs=dict(out=RAW[64:128, :, :], in_=src1),
                 writes=[r_setup["raw"]], dma_sem="tblraw")
            P.op("dve", "tensor_tensor", (RAW[:, :, :], RAW[:, :, :], m01b, ALU.mult),
                 reads=[r_setup["raw"], r_setup["msk"]], writes=[r_setup["raw"]])
            negb5 = NEGM.unsqueeze(1).to_broadcast([128, 5, 64])
            negb7 = NEGM.unsqueeze(1).to_broadcast([128, 7, 64])
            negb3 = NEGM.unsqueeze(1).to_broadcast([128, 3, 64])
            P.op("dve", "tensor_tensor", (TB[:, 0:5, :], RAW[:, 0:5, :], negb5, ALU.add),
                 reads=[r_setup["raw"], r_setup["msk"]], fresh=[r_setup["tb"]])
            P.op("dve", "tensor_tensor", (TB[:, 6:13, :], RAW[:, 5:12, :], negb7, ALU.add),
                 reads=[r_setup["raw"], r_setup["msk"]], writes=[r_setup["tb"]])
            P.op("dve", "tensor_tensor", (TB[:, 14:17, :], RAW[:, 12:15, :], negb3, ALU.add),
                 reads=[r_setup["raw"], r_setup["msk"]], writes=[r_setup["tb"]])
            P.op("dve", "memset", (TB[64:128, 5, :], NEG), writes=[r_setup["tb"]])
            P.op("dve", "memset", (TB[0:64, 13, :], NEG), writes=[r_setup["tb"]])
            P.op("dve", "tensor_copy", (TB[0:64, 5, :], TB[0:64, 4, :]),
                 reads=[r_setup["tb"]], writes=[r_setup["tb"]])
            P.op("dve", "tensor_copy", (TB[64:128, 13, :], TB[64:128, 14, :]),
                 reads=[r_setup["tb"]], writes=[r_setup["tb"]])
            g, hh = h // 2, h % 2
            dstv = tbl_b[l, g, :, hh * NSLOT * 64:(hh + 1) * NSLOT * 64]
            P.op("sp", "dma_start", kwargs=dict(out=dstv, in_=TB[:, :, :].rearrange("p j n -> p (j n)")),
                 reads=[r_setup["tb"]], writes=[r_scr], dma_sem="tblst")

        tjobs = [(l, h) for l in range(2) for h in range(16)]
        ti = 0
        for i in range(len(cast_jobs)):
            emit_cast(i)
            if i % 2 == 1 and ti < len(tjobs):
                emit_table(*tjobs[ti])
                ti += 1
        while ti < len(tjobs):
            emit_table(*tjobs[ti])
            ti += 1
        P.join("sp", [r_scr])

        def ring_load(descr):
            s = state["ring"]
            state["ring"] ^= 1
            first = True
            for dstf, src in descr:
                kw = dict(fresh=[r_ring[s]]) if first else dict(writes=[r_ring[s]])
                P.op("sp", "dma_start", kwargs=dict(out=dstf(s), in_=src), reads=[r_scr], dma_sem="ring%d" % s, **kw)
                first = False
            return s

        def norm_tile(t, gi, out_kind):
            t0 = t * TT
            for c in range(NCH):
                s = state["sq"]
                state["sq"] = (s + 1) % 4
                P.op("act", "activation", (SQ[:, s, :], XT[:, c, t0:t0 + TT], AF.Square),
                     reads=[r_xt[c][t]], writes=[r_sq[s]])
                P.op("pe", "matmul", (PS[7][:, :], ONESM[:, :], SQ[:, s, :]),
                     dict(start=(c == 0), stop=(c == NCH - 1)), reads=[r_sq[s]], writes=[r_bank[7]])
            rs = state["rs"]
            state["rs"] ^= 1
            P.op("act", "activation", (RS[:, rs, :], PS[7][:, :], AF.Ln), dict(bias=EPS_AP),
                 reads=[r_bank[7]], writes=[r_rs[rs]])
            P.op("act", "activation", (RS[:, rs, :], RS[:, rs, :], AF.Exp), dict(scale=-0.5),
                 reads=[r_rs[rs]], writes=[r_rs[rs]])
            for c in range(NCH):
                if out_kind == "hc":
                    out = HC[:, c, t0:t0 + TT]
                    wr = [r_hc[t]]
                else:
                    out = HF[:, c, :]
                    wr = [r_hf]
                P.op("dve", "scalar_tensor_tensor",
                     (out, XT[:, c, t0:t0 + TT], GV[:, gi, c:c + 1], RS[:, rs, :], ALU.mult, ALU.mult),
                     reads=[r_xt[c][t], r_rs[rs]], writes=wr)

        def load_tile(u, t):
            x = xsrc[u["src"]]
            for sub in range(4 * t, 4 * t + 4):
                s = sub % 2
                tok = u["tok0"] + sub * 128
                P.op("sp", "dma_start", kwargs=dict(out=XS[:, s, :], in_=x[u["seq"], tok:tok + 128, :]),
                     fresh=[r_xs[s]], dma_sem="xs%d" % s)
                for half in range(2):
                    bk = 4 + (sub % 2) * 2 + half
                    for cc in range(4):
                        c = half * 4 + cc
                        P.op("pe", "transpose",
                             (PS[bk][:, cc * 128:(cc + 1) * 128], XS[:, s, c * 128:(c + 1) * 128], IDF[:, :]),
                             reads=[r_xs[s]], writes=[r_bank[bk]])
                    outv = XT[:, half * 4:half * 4 + 4, sub * 128:(sub + 1) * 128]
                    inv = PS[bk][:, :].rearrange("p (c n) -> p c n", c=4)
                    wr = [r_xt[half * 4 + cc][t] for cc in range(4)]
                    if half == 0:
                        P.op("act", "copy", (outv, inv), reads=[r_bank[bk]], writes=wr)
                    else:
                        P.op("dve", "tensor_copy", (outv, inv), reads=[r_bank[bk]], writes=wr)

        def phase_load(u):
            arena_phase()
            for t in range(u["R"] * 64 // TT):
                load_tile(u, t)

        def phase_mlp(li, tiles, pre_tile=None):
            gi = 4 + li
            ubank = [0]
            dbank = [0]

            def up(s, t, ab):
                t0 = t * TT
                for m in range(4):
                    bk = ubank[0] % 3
                    ubank[0] += 1
                    for k in range(NCH):
                        P.op("pe", "matmul",
                             (PS[bk][:, :], RING[:, s, k * FS + m * 128:k * FS + (m + 1) * 128], HC[:, k, t0:t0 + TT]),
                             dict(start=(k == 0), stop=(k == NCH - 1)),
                             reads=[r_ring[s], r_hc[t]], writes=[r_bank[bk]])
                    rt = m % 2
                    P.op("act", "activation", (RT[:, rt, :], PS[bk][:, :], AF.Relu),
                         reads=[r_bank[bk]], writes=[r_rt[rt]])
                    P.op("act", "activation", (AT[:, ab, m, :], RT[:, rt, :], AF.Square),
                         reads=[r_rt[rt]], writes=[r_at[ab]])

            def down(s, t, ab):
                t0 = t * TT
                for fo in range(NCH):
                    bk = 3 + dbank[0] % 4
                    dbank[0] += 1
                    for m in range(4):
                        P.op("pe", "matmul",
                             (PS[bk][:, :], RING[:, s, 4096 + m * D + fo * 128:4096 + m * D + (fo + 1) * 128],
                              AT[:, ab, m, :]),
                             dict(start=(m == 0), stop=(m == 3)),
                             reads=[r_ring[s], r_at[ab]], writes=[r_bank[bk]])
                    P.op("dve", "tensor_tensor",
                         (XT[:, fo, t0:t0 + TT], XT[:, fo, t0:t0 + TT], PS[bk][:, :], ALU.add),
                         reads=[r_bank[bk], r_xt[fo][t]], writes=[r_xt[fo][t]])

            slots = {}
            pend = None
            i = 0
            for sl in range(NSL):
                for t in tiles:
                    if sl not in slots:
                        srcu = wup_b[li, :, sl * FS:(sl + 1) * FS].rearrange("(k p) n -> p k n", p=128)
                        srcd = wdn_b[li, sl * FS:(sl + 1) * FS, :].rearrange("(k p) n -> p k n", p=128)
                        slots[sl] = ring_load([
                            (lambda s: RING[:, s, 0:4096].rearrange("p (k n) -> p k n", k=NCH), srcu),
                            (lambda s: RING[:, s, 4096:8192].rearrange("p (k n) -> p k n", k=4), srcd)])
                    if sl == 0:
                        if pre_tile is not None:
                            ti = tiles.index(t)
                            if ti == 0:
                                pre_tile(t)
                            if ti + 1 < len(tiles):
                                pre_tile(tiles[ti + 1])
                        norm_tile(t, gi, "hc")
                    ab = i % 2
                    i += 1
                    up(slots[sl], t, ab)
                    if pend is not None:
                        down(*pend)
                    pend = (slots[sl], t, ab)
            down(*pend)

        def pool_begin(u, li):
            arena_phase()
            j = li // 2
            P.op("sp", "dma_start",
                 kwargs=dict(out=PW.rearrange("p (g k n) -> p g k n", g=4, k=2),
                             in_=pw_b[j].rearrange("g (k p) n -> p g k n", p=128)),
                 reads=[r_scr], fresh=[r_pw], dma_sem="pw")

        def pool_tile(u, li, t):
            j = li // 2
            gi = li
            psi = 8 + j
            T = u["R"] * 64
            ntile = T // TT
            t0 = t * TT
            has_right = (t < ntile - 1)
            nb = TT + (8 if has_right else 0)
            left_true = (u["off"] == 0) and t == 0
            right_true = (u["off"] + u["R"] == u["Rg"]) and t == ntile - 1
            L = TT + 16
            for c in range(NCH):
                sq = state["sq"]
                state["sq"] = (sq + 1) % 4
                P.op("act", "activation", (SQ[:, sq, :], XT[:, c, t0:t0 + TT], AF.Square),
                     reads=[r_xt[c][t]], writes=[r_sq[sq]])
                P.op("pe", "matmul", (PS[7][:, :], ONESM[:, :], SQ[:, sq, :]),
                     dict(start=(c == 0), stop=(c == NCH - 1)), reads=[r_sq[sq]], writes=[r_bank[7]])
            P.op("act", "activation", (RSTDP[:, 8:8 + TT], PS[7][:, :], AF.Ln), dict(bias=EPS_AP),
                 reads=[r_bank[7]], writes=[r_pool["rstdp"]])
            if has_right:
                P.op("act", "activation", (SQH[:, :, :], XT[:, :, t0 + TT:t0 + TT + 8], AF.Square),
                     reads=[r_xt[c][t + 1] for c in range(NCH)], writes=[r_pool["sqh"]])
                for c in range(NCH):
                    P.op("pe", "matmul", (PS[6][:, 0:8], ONESM[:, :], SQH[:, c, :]),
                         dict(start=(c == 0), stop=(c == NCH - 1)), reads=[r_pool["sqh"]], writes=[r_bank[6]])
                P.op("act", "activation", (RSTDP[:, 8 + TT:16 + TT], PS[6][:, 0:8], AF.Ln), dict(bias=EPS_AP),
                     reads=[r_bank[6]], writes=[r_pool["rstdp"]])
            P.op("act", "activation", (RSTDP[:, 8:8 + nb], RSTDP[:, 8:8 + nb], AF.Exp), dict(scale=-0.5),
                 reads=[r_pool["rstdp"]], writes=[r_pool["rstdp"]])
            for g in range(4):
                w = 2 ** (g + 1)
                add_eng = "dve"
                pb = g % 2
                ch = []
                for ki in range(2):
                    c = 2 * g + ki
                    hb = ki
                    ch.append((ki, c, HP2[hb], WA2[hb], WB2[hb],
                               r_pool["hp%d" % hb], r_pool["wa%d" % hb], r_pool["wb%d" % hb]))
                for (ki, c, HPb, WAb, WBb, rhp, rwa, rwb) in ch:
                    if t == 0:
                        P.op("dve", "memset", (HPb[:, 0:8], 0.0), writes=[rhp])
                    else:
                        P.op("dve", "tensor_copy", (HPb[:, 0:8], SV[:, c, 0:8]), reads=[r_pool["sv"]], writes=[rhp])
                    if not has_right:
                        P.op("dve", "memset", (HPb[:, 8 + nb:L], 0.0), writes=[rhp])
                for (ki, c, HPb, WAb, WBb, rhp, rwa, rwb) in ch:
                    P.op("dve", "scalar_tensor_tensor",
                         (HPb[:, 8:8 + nb], XT[:, c, t0:t0 + nb], GV[:, gi, c:c + 1], RSTDP[:, 8:8 + nb],
                          ALU.mult, ALU.mult),
                         reads=[r_xt[c][t]] + ([r_xt[c][t + 1]] if has_right else []) + [r_pool["rstdp"]], writes=[rhp])
                if has_right:
                    for (ki, c, HPb, WAb, WBb, rhp, rwa, rwb) in ch:
                        P.op("dve", "tensor_copy", (SV[:, c, 0:8], HPb[:, TT:TT + 8]), reads=[rhp], writes=[r_pool["sv"]])
                for (ki, c, HPb, WAb, WBb, rhp, rwa, rwb) in ch:
                    P.op(add_eng, "tensor_tensor", (WAb[:, 1:L], HPb[:, 0:L - 1], HPb[:, 1:L], ALU.add),
                         reads=[rhp], writes=[rwa])
                if w >= 4:
                    for (ki, c, HPb, WAb, WBb, rhp, rwa, rwb) in ch:
                        P.op(add_eng, "tensor_tensor", (WBb[:, 2:L - 1], WAb[:, 1:L - 2], WAb[:, 3:L], ALU.add),
                             reads=[rwa], writes=[rwb])
                if w >= 8:
                    for (ki, c, HPb, WAb, WBb, rhp, rwa, rwb) in ch:
                        P.op(add_eng, "tensor_tensor", (WAb[:, 4:L - 3], WBb[:, 2:L - 5], WBb[:, 6:L - 1], ALU.add),
                             reads=[rwb], writes=[rwa])
                if w >= 16:
                    for (ki, c, HPb, WAb, WBb, rhp, rwa, rwb) in ch:
                        P.op(add_eng, "tensor_tensor", (WBb[:, 8:L - 8], WAb[:, 4:L - 12], WAb[:, 12:L - 4], ALU.add),
                             reads=[rwa], writes=[rwb])
                for (ki, c, HPb, WAb, WBb, rhp, rwa, rwb) in ch:
                    if w in (2, 8):
                        cur, curr, oth, othr = WAb, rwa, WBb, rwb
                    else:
                        cur, curr, oth, othr = WBb, rwb, WAb, rwa
                    P.op("dve", "scalar_tensor_tensor",
                         (PL2[:, pb, ki, :], cur[:, 8:8 + TT], 1.0 / w, HPb[:, 8:8 + TT], ALU.mult, ALU.subtract),
                         reads=[curr, rhp], writes=[r_pool["pl%d" % pb]])
                    if left_true:
                        P.op("dve", "tensor_tensor", (oth[:, 0:8], cur[:, 8:16], INVC[:, c, 0:8], ALU.mult),
                             reads=[curr], writes=[othr])
                        P.op("dve", "tensor_tensor", (PL2[:, pb, ki, 0:8], oth[:, 0:8], HPb[:, 8:16], ALU.subtract),
                             reads=[othr, rhp], writes=[r_pool["pl%d" % pb]])
                    if right_true:
                        P.op("dve", "tensor_tensor", (oth[:, 0:8], cur[:, TT:TT + 8], INVC[:, c, 8:16], ALU.mult),
                             reads=[curr], writes=[othr])
                        P.op("dve", "tensor_tensor",
                             (PL2[:, pb, ki, TT - 8:TT], oth[:, 0:8], HPb[:, TT:TT + 8], ALU.subtract),
                             reads=[othr, rhp], writes=[r_pool["pl%d" % pb]])
                for mi in range(2):
                    bk = [3, 4, 5, 6][(2 * g + mi) % 4]
                    for ki in range(2):
                        wo_ = (g * 2 + ki) * 256 + mi * 128
                        P.op("pe", "matmul", (PS[bk][:, :], PW[:, wo_:wo_ + 128], PL2[:, pb, ki, :]),
                             dict(start=(ki == 0), stop=(ki == 1)),
                             reads=[r_pw, r_pool["pl%d" % pb]], writes=[r_bank[bk]])
                    c = 2 * g + mi
                    P.op("dve", "scalar_tensor_tensor",
                         (XT[:, c, t0:t0 + TT], PS[bk][:, :], GV[:, psi, c:c + 1], XT[:, c, t0:t0 + TT],
                          ALU.mult, ALU.add),
                         reads=[r_bank[bk], r_xt[c][t]], writes=[r_xt[c][t]])


        def phase_attn(u, li, ctiles, qrows=None):
            arena_phase()
            j = li // 2
            gi = li
            R = u["R"]
            T = R * 64
            ntile = T // TT
            st = attn_struct(R, u["off"], u["Rg"])
            if qrows is not None:
                st = [[it for it in items if it[0] in qrows] for items in st]
            blk_last = {}
            for p, items in enumerate(st):
                for (rl, sec, jj) in items:
                    blk_last[rl // 8] = p
            P.op("pool", "memset", (VV[:, :, 64:128], 1.0), writes=[r_vv])
            def phase_c(ga, gb):
                cb = 0
                for t in ctiles:
                    t0 = t * TT
                    for fo in range(NCH):
                        bk = [0, 1, 2, 7][cb % 4]
                        cb += 1
                        for qi, gq in enumerate((ga, gb)):
                            P.op("pe", "matmul",
                                 (PS[bk][:, :], WOB[:, gq % 2, fo * 128:(fo + 1) * 128], OTS[gq % 2][:, t0:t0 + TT]),
                                 dict(start=(qi == 0), stop=(qi == 1)),
                                 reads=[r_wob[gq % 2]] + r_ots[gq % 2], writes=[r_bank[bk]])
                        P.op("dve", "tensor_tensor",
                             (XT[:, fo, t0:t0 + TT], XT[:, fo, t0:t0 + TT], PS[bk][:, :], ALU.add),
                             reads=[r_bank[bk], r_xt[fo][t]], writes=[r_xt[fo][t]])

            pending_c = []
            for g in range(ngroups):
                s = ring_load([
                    (lambda s: RING[:, s, 0:1024].rearrange("p (k n) -> p k n", k=NCH),
                     wqkv_b[j, :, g * 128:(g + 1) * 128].rearrange("(k p) n -> p k n", p=128)),
                    (lambda s: RING[:, s, 1024:2048].rearrange("p (k n) -> p k n", k=NCH),
                     wqkv_b[j, :, D + g * 128:D + (g + 1) * 128].rearrange("(k p) n -> p k n", p=128)),
                    (lambda s: RING[:, s, 2048:3072].rearrange("p (k n) -> p k n", k=NCH),
                     wqkv_b[j, :, 2 * D + g * 128:2 * D + (g + 1) * 128].rearrange("(k p) n -> p k n", p=128)),
                    (lambda s: RING[:, s, 4096:4096 + TBLW], tbl_b[j, g, :, :]),
                ])
                for t in range(ntile):
                    t0 = t * TT
                    if g == 0:
                        norm_tile(t, gi, "hc")
                    for bk, wo_ in ((0, 0), (1, 1024)):
                        for k in range(NCH):
                            P.op("pe", "matmul",
                                 (PS[bk][:, :], RING[:, s, wo_ + k * 128:wo_ + (k + 1) * 128], HC[:, k, t0:t0 + TT]),
                                 dict(start=(k == 0), stop=(k == NCH - 1)),
                                 reads=[r_ring[s], r_hc[t]], writes=[r_bank[bk]])
                    P.op("act", "mul", (QT[:, t0:t0 + TT], PS[0][:, :], 0.125), reads=[r_bank[0]], writes=[r_qt])
                    P.op("dve", "tensor_copy", (KT[:, t0:t0 + TT], PS[1][:, :]), reads=[r_bank[1]], writes=[r_kt])
                    for sub in range(4):
                        for k in range(NCH):
                            P.op("pe", "matmul",
                                 (PS[2][:, sub * 128:(sub + 1) * 128], HC[:, k, t0 + sub * 128:t0 + (sub + 1) * 128],
                                  RING[:, s, 2048 + k * 128:2048 + (k + 1) * 128]),
                                 dict(start=(k == 0), stop=(k == NCH - 1)),
                                 reads=[r_ring[s], r_hc[t]], writes=[r_bank[2]])
                    vin = PS[2][:, :].rearrange("p (a n) -> p a n", a=4)
                    vout = VV[:, 4 * t:4 * t + 4, :].rearrange("p a (h n) -> p a h n", h=3)[:, :, 0:3:2, :]
                    vin4 = vin.rearrange("p a (h n) -> p a h n", h=2)
                    P.op("act", "copy", (vout, vin4), reads=[r_bank[2]], writes=[r_vv])

                if pending_c:
                    phase_c(*pending_c.pop())
                P.op("sp", "dma_start", kwargs=dict(out=WOB[:, g % 2, :], in_=wo_b[j, g * 128:(g + 1) * 128, :]),
                     reads=[r_scr], fresh=[r_wob[g % 2]], dma_sem="wob%d" % (g % 2))
                fresh_slot = set()

                def chunks(p):
                    rows = st[p]
                    out = []
                    if not rows:
                        return out
                    if rows[0:8]:
                        out.append((0, 1, 0, rows[0:8]))
                    if rows[8:]:
                        assert len(rows) <= 12
                        out.append((2, 3, 8, rows[8:]))
                    return out

                def qk(p):
                    ps_ = p % 2
                    chs = chunks(p)
                    for (sia, sib, l0, its) in chs:
                        ra = its[0][0]
                        nr = len(its)
                        for hb, si in ((0, sia), (1, sib)):
                            bk = SBK[si]
                            P.op("pe", "matmul",
                                 (PS[bk][:, 0:nr * 64],
                                  KT[hb * 64:(hb + 1) * 64, p * 128:(p + 1) * 128],
                                  QT[hb * 64:(hb + 1) * 64, ra * 64:(ra + nr) * 64]),
                                 dict(start=True, stop=False, skip_group_check=True),
                                 reads=[r_kt, r_qt], writes=[r_bank[bk]])
                    for (sia, sib, l0, its) in chs:
                        ra = its[0][0]
                        runs = []
                        for (rl, sec, jj) in its:
                            sl_ = SLOTS[(sec, jj)]
                            if runs and runs[-1][1] + runs[-1][2] == sl_ and runs[-1][0] + runs[-1][2] == rl:
                                runs[-1][2] += 1
                            else:
                                runs.append([rl, sl_, 1])
                        for hb, si in ((0, sia), (1, sib)):
                            bk = SBK[si]
                            for (rl0, sl0, n) in runs:
                                co = (rl0 - ra) * 64
                                to = 4096 + hb * NSLOT * 64 + sl0 * 64
                                P.op("pe", "matmul",
                                     (PS[bk][:, co:co + n * 64], JB[:, :], RING[:, s, to:to + n * 64]),
                                     dict(start=False, stop=True, skip_group_check=True),
                                     reads=[r_ring[s]], writes=[r_bank[bk]])
                    for (sia, sib, l0, its) in chs:
                        wd = len(its) * 64
                        for si in (sia, sib):
                            bk = SBK[si]
                            P.op("act", "activation", (PT[:, ps_, si, 0:wd], PS[bk][:, 0:wd], AF.Exp),
                                 reads=[r_bank[bk]], writes=[r_pt[ps_][si]])

                def pv(p):
                    ps_ = p % 2
                    for (sia, sib, l0, its) in chunks(p):
                        ra0 = its[0][0]
                        segs = []
                        for (rl, sec, jj) in its:
                            if segs and segs[-1][0] // 8 == rl // 8:
                                segs[-1][1] += 1
                            else:
                                segs.append([rl, 1])
                        for (ra, nr) in segs:
                            blk = ra // 8
                            slot = blk % 2
                            co = (ra % 8) * 64
                            po_ = (ra - ra0) * 64
                            for hb, si in ((0, sia), (1, sib)):
                                bk = (3 + slot) if hb == 0 else (5 + slot)
                                fr = (blk, hb) not in fresh_slot
                                fresh_slot.add((blk, hb))
                                P.op("pe", "matmul",
                                     (PS[bk][:, co:co + nr * 64], VV[:, p, hb * 64:hb * 64 + 128],
                                      PT[:, ps_, si, po_:po_ + nr * 64]),
                                     dict(start=fr, stop=False, skip_group_check=True),
                                     reads=[r_vv, r_pt[ps_][si]], writes=[r_bank[bk]])
                    for blk, lastp in blk_last.items():
                        if lastp != p:
                            continue
                        slot = blk % 2
                        c0 = blk * TT
                        bx, by = 3 + slot, 5 + slot
                        P.op("act", "activation", (RD[0:64, :], PS[bx][64:128, :], AF.Ln),
                             reads=[r_bank[bx]], writes=[r_rd[0]])
                        P.op("act", "activation", (RD[0:64, :], RD[0:64, :], AF.Exp), dict(scale=-1.0),
                             reads=[r_rd[0]], writes=[r_rd[0]])
                        P.op("dve", "tensor_tensor", (OTS[g % 2][0:64, c0:c0 + TT], PS[bx][0:64, :], RD[0:64, :], ALU.mult),
                             reads=[r_bank[bx], r_rd[0]], writes=r_ots[g % 2])
                        P.op("act", "activation", (RD[64:128, :], PS[by][0:64, :], AF.Ln),
                             reads=[r_bank[by]], writes=[r_rd[1]])
                        P.op("act", "activation", (RD[64:128, :], RD[64:128, :], AF.Exp), dict(scale=-1.0),
                             reads=[r_rd[1]], writes=[r_rd[1]])
                        P.op("dve", "tensor_tensor",
                             (OTS[g % 2][64:128, c0:c0 + TT], PS[by][64:128, :], RD[64:128, :], ALU.mult),
                             reads=[r_bank[by], r_rd[1]], writes=r_ots[g % 2])

                npair = len(st)
                if attdbg >= 2:
                    for p in range(npair):
                        qk(p)
                        if p > 0:
                            pv(p - 1)
                    pv(npair - 1)
                elif attdbg == 1:
                    for p in range(npair):
                        qk(p)
                if attdbg < 3:
                    continue

                if g % 2 == 1:
                    pending_c.append((g - 1, g))
            if pending_c:
                phase_c(*pending_c.pop())

        def final_tile(u, t, kst):
            y = ydst[u["dst"]]
            norm_tile(t, 10, "hf")
            for sub in range(4):
                ys = kst[0] % 2
                kst[0] += 1
                for half in range(2):
                    bk = half * 2 + (sub % 2)
                    for cc in range(4):
                        c = half * 4 + cc
                        P.op("pe", "transpose",
                             (PS[bk][:, cc * 128:(cc + 1) * 128], HF[:, c, sub * 128:(sub + 1) * 128], IDF[:, :]),
                             reads=[r_hf], writes=[r_bank[bk]])
                    if half == 0:
                        P.op("act", "copy", (YS[:, ys, 0:512], PS[bk][:, :]), reads=[r_bank[bk]], writes=[r_ys[ys]])
                    else:
                        P.op("dve", "tensor_copy", (YS[:, ys, 512:1024], PS[bk][:, :]),
                             reads=[r_bank[bk]], writes=[r_ys[ys]])
                tok = u["tok0"] + t * TT + sub * 128
                P.op("pool", "dma_start", kwargs=dict(out=y[u["seq"], tok:tok + 128, :], in_=YS[:, ys, :]),
                     reads=[r_ys[ys]], writes=[r_out], dma_sem="yst%d" % ys)

        def phase_final(u, tiles, unext=None):
            arena_phase()
            kst = [0]
            nt_next = (unext["R"] * 64 // TT) if unext is not None else 0
            loaded = set()
            for t in range(nt_next):
                if t not in tiles:
                    load_tile(unext, t)
                    loaded.add(t)
            for t in tiles:
                final_tile(u, t, kst)
                if t < nt_next and t not in loaded:
                    load_tile(unext, t)
                    loaded.add(t)
            for t in range(nt_next):
                if t not in loaded:
                    load_tile(unext, t)

        plist = []
        for li in range(DEPTH):
            plist.append(("pool" if li % 2 == 0 else "attn", li))
            plist.append(("mlp", li))
        plist = plist[:nphases]
        for u in units:
            ntile = u["R"] * 64 // TT
            all_tiles = list(range(ntile))
            out_tiles = list(range(u["out_lo"] // 8, u["out_lo"] // 8 + 4))
            if u is units[0]:
                phase_load(u)
            for kind, li in plist:
                last = (li == DEPTH - 1)
                if kind == "pool":
                    pool_begin(u, li)
                    if ("mlp", li) not in plist:
                        for t in all_tiles:
                            pool_tile(u, li, t)
                elif kind == "attn":
                    qr = set(range(u["out_lo"], u["out_lo"] + 32)) if (last and u["R"] > 32) else None
                    phase_attn(u, li, out_tiles if last else all_tiles, qrows=qr)
                else:
                    pre = None
                    if li % 2 == 0:
                        pre = (lambda t, u=u, li=li: pool_tile(u, li, t))
                    phase_mlp(li, out_tiles if last else all_tiles, pre_tile=pre)
            ui = units.index(u)
            phase_final(u, out_tiles, units[ui + 1] if ui + 1 < len(units) else None)
        P.join("sp", [r_out])
        P.emit(block, sems, dma_sems)
    return nc


_CACHE = {}


def kernel(x_prompt, x_sample, norm_mix, pool_w, pool_scale, w_qkv, rpb, w_o, norm_mlp, w_up, w_down, norm_final):
    if "nc" not in _CACHE:
        _CACHE["nc"] = build_program()
    nc = _CACHE["nc"]
    f32 = np.float32
    hc = host_consts()
    gv = np.concatenate([np.asarray(norm_mix, f32), np.asarray(norm_mlp, f32), np.asarray(pool_scale, f32),
                         np.asarray(norm_final, f32)[None, :]], axis=0)
    gv = np.ascontiguousarray(gv.reshape(11, 8, 128).transpose(2, 0, 1))
    rp = np.asarray(rpb, f32)[:, :, ::-1, ::-1].reshape(2, -1)
    rpbx = np.zeros((2, RPB_L), f32)
    rpbx[:, RPB_PAD:RPB_PAD + rp.shape[1]] = rp
    shared = dict(w_up=np.ascontiguousarray(w_up, dtype=f32), w_down=np.ascontiguousarray(w_down, dtype=f32),
                  w_qkv=np.ascontiguousarray(w_qkv, dtype=f32), w_o=np.ascontiguousarray(w_o, dtype=f32),
                  pool_w=np.ascontiguousarray(pool_w, dtype=f32), gv=gv, rpbx=rpbx,
                  ident=hc["ident"], jmat=hc["jmat"], m01=hc["m01"], negm=hc["negm"], invc=hc["invc"])
    xp = np.asarray(x_prompt, f32)
    xs = np.asarray(x_sample, f32)
    in_maps = []
    for i in range(NCORES):
        m = dict(shared)
        m["xp"] = np.ascontiguousarray(xp[4 * i:4 * i + 4])
        m["xs"] = np.ascontiguousarray(xs[2 * i:2 * i + 2])
        in_maps.append(m)
    res = run_bass_kernel_spmd(nc, in_maps, core_ids=list(range(NCORES)))
    yp = np.concatenate([np.asarray(r["yp"], f32) for r in res.results], axis=0)
    ys = np.concatenate([np.asarray(r["ys"], f32) for r in res.results], axis=0)
    return (yp, ys)
```
